# Optimizing a Trainium2 kernel written in Bass

```python
import jax, jax.numpy as jnp
from jax import lax
import numpy as np

D_MODEL = 1024
BATCH = 8
SEQ = 4096
DEPTH = 4

N_MIXERS = 2
N_POOL_LAYERS = (DEPTH + 1) // 2
N_MLA_LAYERS = DEPTH // 2

POOL_WINDOWS = (2, 4, 8, 16)
N_POOL_GROUPS = len(POOL_WINDOWS)
POOL_GROUP = D_MODEL // N_POOL_GROUPS

N_HEADS = D_MODEL // 128
QK_NOPE_DIM = 128
QK_ROPE_DIM = 64
V_HEAD_DIM = 128
Q_LORA_RANK = 3 * D_MODEL // 8
KV_LORA_RANK = D_MODEL // 8
QK_HEAD_DIM = QK_NOPE_DIM + QK_ROPE_DIM
ROPE_THETA = 10000.0
Q_BLOCK = 128

D_FF = ((8 * D_MODEL // 3 + 255) // 256) * 256

RMS_EPS = 1e-6

kernel_name = "hybrid_pool_mla_swiglu_trunk"


def rms_norm(x, g):
    xf = x.astype(jnp.float32)
    y = xf * lax.rsqrt(jnp.mean(xf * xf, axis=-1, keepdims=True) + RMS_EPS)
    return (y * g.astype(jnp.float32)).astype(x.dtype)


def pool_mixer(h, w, scale):
    B, S, D = h.shape
    hf = h.astype(jnp.float32).reshape(B, S, N_POOL_GROUPS, POOL_GROUP)
    cs = jnp.cumsum(hf, axis=1)
    n_avail = jnp.arange(1, S + 1, dtype=jnp.float32)
    outs = []
    for g, win in enumerate(POOL_WINDOWS):
        c = cs[:, :, g]
        lag = jnp.pad(c[:, :-win], ((0, 0), (win, 0), (0, 0)))
        mean = (c - lag) / jnp.minimum(n_avail, float(win))[:, None]
        outs.append(mean - hf[:, :, g])
    p = jnp.stack(outs, axis=2).astype(h.dtype)
    y = jnp.einsum('bsgc,gcd->bsgd', p, w).reshape(B, S, D)
    return y * scale


def rope_tables(positions):
    inv = 1.0 / (ROPE_THETA ** (jnp.arange(0, QK_ROPE_DIM, 2, dtype=jnp.float32) / QK_ROPE_DIM))
    ang = positions.astype(jnp.float32)[..., None] * inv
    return jnp.cos(ang), jnp.sin(ang)


def apply_rope(x, cos, sin):
    xf = x.astype(jnp.float32)
    half = QK_ROPE_DIM // 2
    x1, x2 = xf[..., :half], xf[..., half:]
    return jnp.concatenate([x1 * cos - x2 * sin, x2 * cos + x1 * sin], axis=-1).astype(x.dtype)


def causal_attention(q, k, v):
    B, S, H, Dq = q.shape
    nb = S // Q_BLOCK
    qb = q.reshape(B, nb, Q_BLOCK, H, Dq).transpose(1, 0, 2, 3, 4)
    key_pos = jnp.arange(S)
    sm_scale = Dq ** -0.5

    def one_block(args):
        blk, qi = args
        s = jnp.einsum('bqhd,bkhd->bhqk', qi, k).astype(jnp.float32) * sm_scale
        q_pos = blk * Q_BLOCK + jnp.arange(Q_BLOCK)
        mask = key_pos[None, :] <= q_pos[:, None]
        s = jnp.where(mask, s, -jnp.inf)
        p = jax.nn.softmax(s, axis=-1).astype(v.dtype)
        return jnp.einsum('bhqk,bkhd->bqhd', p, v)

    o = lax.map(one_block, (jnp.arange(nb), qb))
    return o.transpose(1, 0, 2, 3, 4).reshape(B, S, H, v.shape[-1])


def mla(h, cos, sin, w_down, q_norm, w_uq, kv_norm, w_ukv, w_o):
    B, S, _ = h.shape
    d = h @ w_down
    cq = rms_norm(d[..., :Q_LORA_RANK], q_norm)
    ckv = rms_norm(d[..., Q_LORA_RANK:Q_LORA_RANK + KV_LORA_RANK], kv_norm)
    k_rope = apply_rope(d[..., Q_LORA_RANK + KV_LORA_RANK:], cos, sin)
    q = (cq @ w_uq).reshape(B, S, N_HEADS, QK_HEAD_DIM)
    q_rope = apply_rope(q[..., QK_NOPE_DIM:], cos[:, :, None], sin[:, :, None])
    q = jnp.concatenate([q[..., :QK_NOPE_DIM], q_rope], axis=-1)
    kv = (ckv @ w_ukv).reshape(B, S, N_HEADS, QK_NOPE_DIM + V_HEAD_DIM)
    k = jnp.concatenate(
        [kv[..., :QK_NOPE_DIM],
         jnp.broadcast_to(k_rope[:, :, None, :], (B, S, N_HEADS, QK_ROPE_DIM))], axis=-1)
    v = kv[..., QK_NOPE_DIM:]
    o = causal_attention(q, k, v)
    return o.reshape(B, S, N_HEADS * V_HEAD_DIM) @ w_o


def swiglu(h, w_gate, w_up, w_down):
    return (jax.nn.silu(h @ w_gate) * (h @ w_up)) @ w_down


def setup_inputs(seed: int = 0) -> dict:
    key = jax.random.key(seed)
    ks = jax.random.split(key, 16)
    f32 = jnp.float32

    def nrm(k, shape, fan_in):
        return jax.random.normal(k, shape, f32) * (fan_in ** -0.5)

    def gain(k, shape):
        return 1.0 + 0.02 * jax.random.normal(k, shape, f32)

    x = jax.random.normal(ks[0], (BATCH, SEQ, D_MODEL), f32)
    positions = jnp.broadcast_to(jnp.arange(SEQ, dtype=jnp.int32)[None, :], (BATCH, SEQ))
    return {
        "x": x,
        "positions": positions,
        "norm_mix": gain(ks[1], (DEPTH, D_MODEL)),
        "norm_ffn": gain(ks[2], (DEPTH, D_MODEL)),
        "norm_final": gain(ks[3], (D_MODEL,)),
        "pool_w": nrm(ks[4], (N_POOL_LAYERS, N_POOL_GROUPS, POOL_GROUP, POOL_GROUP), POOL_GROUP),
        "pool_scale": gain(ks[5], (N_POOL_LAYERS, D_MODEL)),
        "mla_w_down": nrm(ks[6], (N_MLA_LAYERS, D_MODEL, Q_LORA_RANK + KV_LORA_RANK + QK_ROPE_DIM), D_MODEL),
        "mla_q_norm": gain(ks[7], (N_MLA_LAYERS, Q_LORA_RANK)),
        "mla_w_uq": nrm(ks[8], (N_MLA_LAYERS, Q_LORA_RANK, N_HEADS * QK_HEAD_DIM), Q_LORA_RANK),
        "mla_kv_norm": gain(ks[9], (N_MLA_LAYERS, KV_LORA_RANK)),
        "mla_w_ukv": nrm(ks[10], (N_MLA_LAYERS, KV_LORA_RANK, N_HEADS * (QK_NOPE_DIM + V_HEAD_DIM)), KV_LORA_RANK),
        "mla_w_o": nrm(ks[11], (N_MLA_LAYERS, N_HEADS * V_HEAD_DIM, D_MODEL), N_HEADS * V_HEAD_DIM),
        "ffn_w_gate": nrm(ks[12], (DEPTH, D_MODEL, D_FF), D_MODEL),
        "ffn_w_up": nrm(ks[13], (DEPTH, D_MODEL, D_FF), D_MODEL),
        "ffn_w_down": nrm(ks[14], (DEPTH, D_FF, D_MODEL), D_FF),
    }


def reference(x, positions, norm_mix, norm_ffn, norm_final, pool_w, pool_scale,
              mla_w_down, mla_q_norm, mla_w_uq, mla_kv_norm, mla_w_ukv, mla_w_o,
              ffn_w_gate, ffn_w_up, ffn_w_down):
    cos, sin = rope_tables(positions)
    for i in range(DEPTH):
        h = rms_norm(x, norm_mix[i])
        j = i // N_MIXERS
        if i % N_MIXERS == 0:
            x = x + pool_mixer(h, pool_w[j], pool_scale[j])
        else:
            x = x + mla(h, cos, sin, mla_w_down[j], mla_q_norm[j], mla_w_uq[j],
                        mla_kv_norm[j], mla_w_ukv[j], mla_w_o[j])
        h = rms_norm(x, norm_ffn[i])
        x = x + swiglu(h, ffn_w_gate[i], ffn_w_up[i], ffn_w_down[i])
    return rms_norm(x, norm_final)
```

```python
import contextlib
import math
import numpy as np
import concourse.bass as bass
import concourse.mybir as mybir
from concourse.bass_utils import run_bass_kernel_spmd

F32 = mybir.dt.float32
BF16 = mybir.dt.bfloat16
I32 = mybir.dt.int32
ALU = mybir.AluOpType
AF = mybir.ActivationFunctionType

D = 1024
DC = 8
DFF = 2816
FC = 22
NH = 8
QL = 384
KVL = 128
ROPE = 64
TT = 512
EPS = 1e-6
DEPTH = 4
EPOCH = 20000
N_CORES = 8


class Op:
    __slots__ = ("eng", "fn", "deps", "needs_inc", "sig", "is_dma", "dsem")


class Prog:
    ENGS = ["sync", "tensor", "vector", "scalar", "gpsimd"]

    def __init__(self, nc, stack):
        self.nc = nc
        self.stack = stack
        self.ops = {e: [] for e in self.ENGS}
        self.lastw = {}
        self.readers = {}
        self.dma_sems = {}
        self.eng_sems = {}
        self.pending = {}

    def barrier(self):
        deps = []
        for e in self.ENGS:
            for op in reversed(self.ops[e]):
                if not op.is_dma and op.fn is not None:
                    deps.append(op)
                    break
        seen = set()
        for e in self.ENGS:
            for op in reversed(self.ops[e]):
                if op.is_dma and op.dsem not in seen:
                    seen.add(op.dsem)
                    if not (isinstance(op.dsem, tuple) and op.dsem[0] == "cv"):
                        deps.append(op)
        self.pending = {e: list(deps) for e in self.ENGS}

    def _new_sem(self, name):
        return self.stack.enter_context(self.nc.semaphore(name))

    def add(self, eng, fn, reads=(), writes=(), dsem=None, nowaw=False, after=()):
        op = Op()
        op.eng = eng
        op.fn = fn
        op.deps = set()
        op.needs_inc = False
        op.sig = None
        op.is_dma = dsem is not None
        op.dsem = dsem
        for k in reads:
            w = self.lastw.get(k)
            if w is not None:
                op.deps.add(w)
        for k in writes:
            w = self.lastw.get(k)
            if w is not None and (w.is_dma or w.eng != eng) and not nowaw:
                op.deps.add(w)
            for r in self.readers.get(k, ()):
                if r is not op and (r.is_dma or r.eng != eng):
                    op.deps.add(r)
        if eng == "tensor":
            op.deps = {d for d in op.deps if d.is_dma or d.eng != "tensor"}
        op.deps.update(after)
        pend = self.pending.pop(eng, None)
        if pend:
            op.deps.update(d for d in pend if d.is_dma or d.eng != eng)
        for d in op.deps:
            d.needs_inc = True
        for k in reads:
            self.readers.setdefault(k, []).append(op)
        for k in writes:
            self.lastw[k] = op
            self.readers[k] = []
        if op.is_dma:
            ent = self.dma_sems.get(dsem)
            if ent is None:
                ent = [self._new_sem("d%d_%s" % (len(self.dma_sems), "".join(ch for ch in str(dsem) if ch.isalnum()))), 0]
                self.dma_sems[dsem] = ent
            ent[1] += 16
            op.sig = (ent[0], ent[1])
        self.ops[eng].append(op)
        return op

    def _assign(self):
        for e in self.ENGS:
            cnt = 0
            for op in self.ops[e]:
                if op.is_dma or not op.needs_inc:
                    continue
                ep = cnt // EPOCH
                key = (e, ep)
                if key not in self.eng_sems:
                    self.eng_sems[key] = self._new_sem("c_%s_%d" % (e, ep))
                cnt += 1
                op.sig = (self.eng_sems[key], cnt - ep * EPOCH)

    def emit(self):
        self._assign()
        nc = self.nc
        with nc.Block() as block:
            for e in self.ENGS:
                ops = self.ops[e]

                def body(eng, ops=ops):
                    waited = {}
                    for op in ops:
                        need = {}
                        for d in op.deps:
                            sem, val = d.sig
                            if need.get(id(sem), (None, 0))[1] < val:
                                need[id(sem)] = (sem, val)
                        for sid, (sem, val) in need.items():
                            if waited.get(sid, 0) < val:
                                eng.wait_ge(sem, val)
                                waited[sid] = val
                        if op.fn is None:
                            continue
                        ins = op.fn(eng)
                        if op.is_dma:
                            ins.then_inc(op.sig[0], 16)
                        elif op.needs_inc:
                            ins.then_inc(op.sig[0], 1)

                getattr(block, e)(body)


class Arena:
    def __init__(self, ap, nwords):
        self.ap = ap
        self.n = nwords
        self.off = 0

    def mark(self):
        return self.off

    def reset(self, m):
        self.off = m

    def f32(self, ncols, parts=128):
        a = self.ap[0:parts, self.off:self.off + ncols]
        self.off += ncols
        assert self.off <= self.n, ("arena overflow", self.off, self.n)
        return a

    def bf16(self, ncols, parts=128):
        w = (ncols + 1) // 2
        a = self.ap[0:parts, self.off:self.off + w].bitcast(BF16)
        self.off += w
        assert self.off <= self.n, ("arena overflow", self.off, self.n)
        return a

    def i32(self, ncols, parts=128):
        a = self.ap[0:parts, self.off:self.off + ncols].bitcast(I32)
        self.off += ncols
        assert self.off <= self.n
        return a


class Cx:
    pass


def mm(cx, out, lhsT, rhs, start, stop, reads, writes):
    return cx.p.add("tensor", lambda e: e.matmul(out, lhsT, rhs, start=start, stop=stop), reads=reads, writes=writes)


def norm_squares(cx, src_fn, src_keys, nch, sq, sqkey):
    p = cx.p
    for c in range(nch):
        p.add("scalar", lambda e, c=c: e.activation(out=sq[:, c, :], in_=src_fn(c), func=AF.Square),
              reads=[src_keys[c]], writes=[(sqkey, c)])


def norm_rstd(cx, nch, sq, sqkey, bank, rstd, rstdkey, dim):
    p = cx.p
    for c in range(nch):
        mm(cx, cx.ps[bank][:, :], cx.ones[:, :], sq[:, c, :], c == 0, c == nch - 1,
           reads=[(sqkey, c), "ones"], writes=[("ps", bank)])
    p.add("scalar", lambda e: e.activation(out=rstd, in_=cx.ps[bank][:, :], func=AF.Ln, bias=cx.epsc, scale=1.0 / dim),
          reads=[("ps", bank), "epsc"], writes=[rstdkey])
    p.add("scalar", lambda e: e.activation(out=rstd, in_=rstd, func=AF.Exp, scale=-0.5), reads=[rstdkey], writes=[rstdkey])


def norm_apply_res(cx, t, gcol0, rstd, rstdkey, hT, hkey):
    p = cx.p
    for c in range(DC):
        p.add("vector",
              lambda e, c=c: e.scalar_tensor_tensor(out=hT[:, c, :], in0=cx.res[:, c, t * TT:(t + 1) * TT],
                                                    scalar=cx.vecs[:, gcol0 + c:gcol0 + c + 1], in1=rstd,
                                                    op0=ALU.mult, op1=ALU.mult),
              reads=[("res", c, t), rstdkey, "vecs"], writes=[(hkey, c)])


def ffn_phase(cx, l):
    p = cx.p
    p.barrier()
    cx.arena.n = cx.arena_full
    ar = cx.arena
    m0 = ar.mark()
    NT = cx.NT
    sq = ar.bf16(DC * TT).rearrange("p (c n) -> p c n", c=DC)
    rstd = ar.f32(TT)
    hT = ar.bf16(DC * TT).rearrange("p (c n) -> p c n", c=DC)
    gT = ar.bf16(FC * TT).rearrange("p (c n) -> p c n", c=FC)
    stmp = [ar.f32(TT) for _ in range(2)]
    NW1, NW2 = 4, 3
    wgu_s = [ar.bf16(DC * 2 * 128).rearrange("p (c g n) -> p c g n", c=DC, g=2) for _ in range(NW1)]
    wd_s = [ar.bf16(FC * 128).rearrange("p (c n) -> p c n", c=FC) for _ in range(NW2)]
    gcol = cx.vcol["norm_ffn"][l]
    wgu_d = cx.wbf["wgu"][l]
    wd_d = cx.wbf["wd"][l]
    uid = ("ffn", l)

    def squares(t):
        norm_squares(cx, lambda c, t=t: cx.res[:, c, t * TT:(t + 1) * TT], [("res", c, t) for c in range(DC)], DC, sq, "sq")

    def rest(t):
        norm_rstd(cx, DC, sq, "sq", 6, rstd, "rstd", float(D))
        norm_apply_res(cx, t, gcol, rstd, "rstd", hT, "hT")

    squares(0)
    rest(0)
    n1 = 0
    n2 = 0
    for t in range(NT):
        for fc in range(FC):
            s = n1 % NW1
            n1 += 1
            p.add("sync", lambda e, s=s, fc=fc: e.dma_start(out=wgu_s[s].rearrange("p c g n -> p (c g n)"), in_=wgu_d[fc]),
                  reads=[("wgu_d", l, fc // 11)], writes=[("wgu_s", s)], dsem=("wgu_s", s))
            bg = fc % 2
            bu = 2 + fc % 2
            for dc in range(DC):
                mop = mm(cx, cx.ps[bg][:, :], wgu_s[s][:, dc, 0, :], hT[:, dc, :], dc == 0, dc == DC - 1,
                         reads=[("wgu_s", s), ("hT", dc)], writes=[("ps", bg)])
                if dc == 0 and cx.pending_casts and (t * FC + fc) % 3 == 0:
                    cx.pending_casts.pop(0)([mop])
            for dc in range(DC):
                mm(cx, cx.ps[bu][:, :], wgu_s[s][:, dc, 1, :], hT[:, dc, :], dc == 0, dc == DC - 1,
                   reads=[("wgu_s", s), ("hT", dc)], writes=[("ps", bu)])
            st = stmp[fc % 2]
            p.add("scalar", lambda e, st=st, bg=bg: e.activation(out=st, in_=cx.ps[bg][:, :], func=AF.Silu),
                  reads=[("ps", bg)], writes=[("stmp", fc % 2)])
            p.add("vector", lambda e, st=st, bu=bu, fc=fc: e.tensor_tensor(out=gT[:, fc, :], in0=st, in1=cx.ps[bu][:, :], op=ALU.mult),
                  reads=[("stmp", fc % 2), ("ps", bu)], writes=[("gT", fc)])
            if fc == 11 and t + 1 < NT:
                squares(t + 1)
        if t + 1 < NT:
            rest(t + 1)
        for j in range(DC):
            s = n2 % NW2
            n2 += 1
            p.add("sync", lambda e, s=s, j=j: e.dma_start(out=wd_s[s].rearrange("p c n -> p (c n)"), in_=wd_d[j]),
                  reads=[("wd_d", l)], writes=[("wd_s", s)], dsem=("wd_s", s))
            b = 4 + j % 2
            for fc in range(FC):
                mm(cx, cx.ps[b][:, :], wd_s[s][:, fc, :], gT[:, fc, :], fc == 0, fc == FC - 1,
                   reads=[("wd_s", s), ("gT", fc)], writes=[("ps", b)])
            rs = cx.res[:, j, t * TT:(t + 1) * TT]
            p.add("vector", lambda e, rs=rs, b=b: e.tensor_tensor(out=rs, in0=rs, in1=cx.ps[b][:, :], op=ALU.add),
                  reads=[("ps", b), ("res", j, t)], writes=[("res", j, t)])
    ar.reset(m0)


POOL_WINDOWS = (2, 4, 8, 16)
HALO = 16


def pool_phase(cx, l, j):
    p = cx.p
    p.barrier()
    ar = cx.arena
    m0 = ar.mark()
    NT = cx.NT
    W = HALO + TT
    sq = ar.bf16(DC * TT).rearrange("p (c n) -> p c n", c=DC)
    rstd = ar.f32(TT)
    hb = ar.f32(DC * W).rearrange("p (c n) -> p c n", c=DC)
    NA = 4
    ab = [ar.f32(W) for _ in range(NA)]
    pT = ar.bf16(DC * TT).rearrange("p (c n) -> p c n", c=DC)
    wp = ar.bf16(4 * 2 * 256).rearrange("p (g c n) -> p g c n", g=4, c=2)
    ftmp = ar.f32(HALO)
    gcol = cx.vcol["norm_mix"][l]
    scol = cx.vcol["pool_scale"][j]
    p.add("sync", lambda e: e.dma_start(out=wp.rearrange("p g c n -> p (g c n)"), in_=cx.wbf["wp"][j]),
          reads=[("wp_d", j)], writes=["wp_s"], dsem="wp_s")
    for c in range(DC):
        p.add("vector", lambda e, c=c: e.memset(hb[:, c, 0:HALO], 0.0), writes=[("hb", c)])
    na = 0
    for t in range(NT):
        norm_squares(cx, lambda c, t=t: cx.res[:, c, t * TT:(t + 1) * TT], [("res", c, t) for c in range(DC)], DC, sq, "sq")
        norm_rstd(cx, DC, sq, "sq", 6, rstd, "rstd", float(D))
        for c in range(DC):
            p.add("vector",
                  lambda e, c=c, t=t: e.scalar_tensor_tensor(out=hb[:, c, HALO:W], in0=cx.res[:, c, t * TT:(t + 1) * TT],
                                                        scalar=cx.vecs[:, gcol + c:gcol + c + 1], in1=rstd,
                                                        op0=ALU.mult, op1=ALU.mult),
                  reads=[("res", c, t), "rstd", "vecs"], writes=[("hb", c)])
        for c in range(DC):
            g = c // 2
            win = POOL_WINDOWS[g]
            nlev = g + 1
            src = hb[:, c, :]
            srck = ("hb", c)
            lo = 0
            for lev in range(nlev):
                sh = 1 << lev
                lo = lo + sh
                a = ab[na % NA]
                ak = ("ab", na % NA)
                na += 1
                p.add("vector", lambda e, a=a, src=src, lo=lo, sh=sh: e.tensor_tensor(out=a[:, lo:W], in0=src[:, lo:W], in1=src[:, lo - sh:W - sh], op=ALU.add),
                      reads=[srck], writes=[ak])
                src = a
                srck = ak
            p.add("vector", lambda e, src=src, c=c, win=win: e.scalar_tensor_tensor(out=pT[:, c, :], in0=src[:, HALO:W], scalar=1.0 / win, in1=hb[:, c, HALO:W],
                                                                                op0=ALU.mult, op1=ALU.subtract),
                  reads=[srck, ("hb", c)], writes=[("pT", c)])
            if t == 0:
                n = win - 1
                p.add("vector", lambda e, src=src, n=n: e.tensor_tensor(out=ftmp[:, 0:n], in0=src[:, HALO:HALO + n], in1=cx.invn[:, 0:n], op=ALU.mult),
                      reads=[srck, "invn"], writes=["ftmp"])
                p.add("vector", lambda e, c=c, n=n: e.tensor_tensor(out=pT[:, c, 0:n], in0=ftmp[:, 0:n], in1=hb[:, c, HALO:HALO + n], op=ALU.subtract),
                      reads=["ftmp", ("hb", c)], writes=[("pT", c)])
        if t + 1 < NT:
            for c in range(DC):
                p.add("vector", lambda e, c=c: e.tensor_copy(out=hb[:, c, 0:HALO], in_=hb[:, c, TT:W]),
                      reads=[("hb", c)], writes=[("hb", c)])
        for oc in range(DC):
            g = oc // 2
            o2 = oc % 2
            b = oc % 4
            for cc in range(2):
                mm(cx, cx.ps[b][:, :], wp[:, g, cc, o2 * 128:(o2 + 1) * 128], pT[:, 2 * g + cc, :], cc == 0, cc == 1,
                   reads=["wp_s", ("pT", 2 * g + cc)], writes=[("ps", b)])
            rs = cx.res[:, oc, t * TT:(t + 1) * TT]
            p.add("vector", lambda e, rs=rs, b=b, oc=oc: e.scalar_tensor_tensor(out=rs, in0=cx.ps[b][:, :], scalar=cx.vecs[:, scol + oc:scol + oc + 1], in1=rs,
                                                                                op0=ALU.mult, op1=ALU.add),
                  reads=[("ps", b), ("res", oc, t), "vecs"], writes=[("res", oc, t)])
    ar.reset(m0)


def rope_tables(cx, top, posall):
    p = cx.p
    ar = top
    S = cx.S
    C1 = 6.28125
    C2 = 2 * math.pi - 6.28125
    ang, kf, sn, cs = ar.f32(TT, 64), ar.f32(TT, 64), ar.f32(TT, 64), ar.f32(TT, 64)
    ki = ar.i32(TT, 64)
    tab_ops = []
    for t in range(cx.NT):
        posi = posall[:, t * TT:(t + 1) * TT]
        u = ("rt", 0)
        p.add("vector", lambda e, ang=ang, posi=posi: e.tensor_copy(out=ang, in_=posi), reads=["posall"], writes=[(u, "ang")])
        p.add("vector", lambda e, ang=ang: e.tensor_scalar(out=ang, in0=ang, scalar1=cx.cst64[:, 0:1], scalar2=None, op0=ALU.mult),
              reads=[(u, "ang"), "cst64"], writes=[(u, "ang")])
        p.add("vector", lambda e, ang=ang, kf=kf: e.tensor_single_scalar(out=kf, in_=ang, scalar=1.0 / (2 * math.pi), op=ALU.mult),
              reads=[(u, "ang")], writes=[(u, "kf")])
        p.add("vector", lambda e, ki=ki, kf=kf: e.tensor_copy(out=ki, in_=kf), reads=[(u, "kf")], writes=[(u, "ki")])
        p.add("vector", lambda e, ki=ki, kf=kf: e.tensor_copy(out=kf, in_=ki), reads=[(u, "ki")], writes=[(u, "kf")])
        p.add("vector", lambda e, ang=ang, kf=kf: e.scalar_tensor_tensor(out=ang, in0=kf, scalar=-C1, in1=ang, op0=ALU.mult, op1=ALU.add),
              reads=[(u, "kf"), (u, "ang")], writes=[(u, "ang")])
        p.add("vector", lambda e, ang=ang, kf=kf: e.scalar_tensor_tensor(out=ang, in0=kf, scalar=-C2, in1=ang, op0=ALU.mult, op1=ALU.add),
              reads=[(u, "kf"), (u, "ang")], writes=[(u, "ang")])
        p.add("vector", lambda e, ang=ang, kf=kf: e.tensor_scalar(out=kf, in0=ang, scalar1=math.pi, scalar2=-2 * math.pi, op0=ALU.is_gt, op1=ALU.mult),
              reads=[(u, "ang")], writes=[(u, "kf")])
        p.add("vector", lambda e, ang=ang, kf=kf: e.tensor_tensor(out=ang, in0=ang, in1=kf, op=ALU.add),
              reads=[(u, "ang"), (u, "kf")], writes=[(u, "ang")])
        p.add("scalar", lambda e, ang=ang, sn=sn: e.activation(out=sn, in_=ang, func=AF.Sin), reads=[(u, "ang")], writes=[(u, "sn")])
        p.add("vector", lambda e, sn=sn: e.tensor_scalar(out=sn, in0=sn, scalar1=cx.cst64[:, 1:2], scalar2=None, op0=ALU.mult),
              reads=[(u, "sn"), "cst64"], writes=[(u, "sn")])
        p.add("vector", lambda e, ang=ang: e.tensor_single_scalar(out=ang, in_=ang, scalar=math.pi / 2, op=ALU.add),
              reads=[(u, "ang"), (u, "sn")], writes=[(u, "ang")])
        p.add("vector", lambda e, ang=ang, kf=kf: e.tensor_scalar(out=kf, in0=ang, scalar1=math.pi, scalar2=-2 * math.pi, op0=ALU.is_gt, op1=ALU.mult),
              reads=[(u, "ang")], writes=[(u, "kf")])
        p.add("vector", lambda e, ang=ang, kf=kf: e.tensor_tensor(out=ang, in0=ang, in1=kf, op=ALU.add),
              reads=[(u, "ang"), (u, "kf")], writes=[(u, "ang")])
        p.add("scalar", lambda e, ang=ang, cs=cs: e.activation(out=cs, in_=ang, func=AF.Sin), reads=[(u, "ang")], writes=[(u, "cs")])
        tab_ops.append(p.add("sync", lambda e, cs=cs, t=t: e.dma_start(out=cx.tab[0][:, t * TT:(t + 1) * TT], in_=cs),
                             reads=[(u, "cs")], writes=[("tab", t, 0)], dsem=("tabw", t, 0)))
        tab_ops.append(p.add("sync", lambda e, sn=sn, t=t: e.dma_start(out=cx.tab[1][:, t * TT:(t + 1) * TT], in_=sn),
                             reads=[(u, "sn")], writes=[("tab", t, 1)], dsem=("tabw", t, 1)))
    return tab_ops


def mla_phase(cx, l, j):
    p = cx.p
    p.barrier()
    cx.arena.n = cx.arena_full
    ar = cx.arena
    m0 = ar.mark()
    NT = cx.NT
    S = cx.S
    NKC = S // 128
    sm_scale = float((128 + ROPE) ** -0.5)
    ckvn = ar.bf16(S)
    krope = ar.bf16(S)
    m1 = ar.mark()
    sq = ar.bf16(DC * TT).rearrange("p (c n) -> p c n", c=DC)
    rstd = ar.f32(TT)
    rstd2 = ar.f32(TT)
    hT = ar.bf16(DC * TT).rearrange("p (c n) -> p c n", c=DC)
    cq_sb = ar.f32(3 * TT).rearrange("p (c n) -> p c n", c=3)
    ckv_sb = ar.f32(TT)
    cqn_t = [ar.bf16(3 * TT).rearrange("p (c n) -> p c n", c=3) for _ in range(2)]
    tabA = [ar.f32(2 * TT, 64).rearrange("p (c n) -> p c n", c=2) for _ in range(2)]
    rt1 = ar.f32(TT, 64)
    rt2 = ar.f32(TT, 64)
    wdn = ar.bf16(DC * 640).rearrange("p (c n) -> p c n", c=DC)
    gcol = cx.vcol["norm_mix"][l]
    qcol = cx.vcol["q_norm"][j]
    kvcol = cx.vcol["kv_norm"][j]
    p.add("vector", lambda e: e.memset(krope[64:128, :], 0.0), writes=[("krope", t) for t in range(NT)])
    p.add("sync", lambda e: e.dma_start(out=wdn.rearrange("p c n -> p (c n)"), in_=cx.wbf["wdn"][j]),
          reads=[("wdn_d", j)], writes=["wdn_s"], dsem="wdn_s")
    sq2 = ar.bf16(4 * TT).rearrange("p (c n) -> p c n", c=4)
    rstd3 = rstd2

    def stage1(t):
        ts = slice(t * TT, (t + 1) * TT)
        norm_squares(cx, lambda c, ts=ts: cx.res[:, c, ts], [("res", c, t) for c in range(DC)], DC, sq, "sq")
        norm_rstd(cx, DC, sq, "sq", 6, rstd, "rstd", float(D))
        norm_apply_res(cx, t, gcol, rstd, "rstd", hT, "hT")

    stage1(0)
    for t in range(NT):
        ts = slice(t * TT, (t + 1) * TT)
        tb = tabA[t % 2]
        p.add("sync", lambda e, tb=tb, ts=ts: e.dma_start(out=tb[:, 0, :], in_=cx.tab[0][:, ts]),
              reads=[("tab", t, 0), ("tab", t, 1)], writes=[("tabA", t % 2)], dsem=("tabA", t % 2))
        p.add("sync", lambda e, tb=tb, ts=ts: e.dma_start(out=tb[:, 1, :], in_=cx.tab[1][:, ts]),
              reads=[("tab", t, 0), ("tab", t, 1)], writes=[("tabA", t % 2)], dsem=("tabA", t % 2))
        outs = [(0, 0, 128), (1, 128, 128), (2, 256, 128), (3, 384, 128), (4, 512, 64), (5, 576, 64)]
        for (b, c0, m) in outs:
            for dc in range(DC):
                mm(cx, cx.ps[b][0:m, :], wdn[:, dc, c0:c0 + m], hT[:, dc, :], dc == 0, dc == DC - 1,
                   reads=["wdn_s", ("hT", dc)], writes=[("ps", b)])
        if t + 1 < NT:
            stage1(t + 1)
        for c in range(3):
            p.add("scalar", lambda e, c=c: e.activation(out=cq_sb[:, c, :], in_=cx.ps[c][:, :], func=AF.Copy),
                  reads=[("ps", c)], writes=[("cq_sb", c)])
        p.add("scalar", lambda e: e.activation(out=ckv_sb, in_=cx.ps[3][:, :], func=AF.Copy),
              reads=[("ps", 3)], writes=["ckv_sb"])
        p.add("vector", lambda e, tb=tb: e.tensor_tensor(out=rt1, in0=cx.ps[4][0:64, :], in1=tb[:, 0, :], op=ALU.mult),
              reads=[("ps", 4), ("tabA", t % 2)], writes=["rt1"])
        p.add("vector", lambda e, tb=tb: e.tensor_tensor(out=rt2, in0=cx.ps[5][0:64, :], in1=tb[:, 1, :], op=ALU.mult),
              reads=[("ps", 5), ("tabA", t % 2)], writes=["rt2"])
        p.add("vector", lambda e, ts=ts: e.tensor_tensor(out=krope[0:64, ts], in0=rt1, in1=rt2, op=ALU.add),
              reads=["rt1", "rt2"], writes=[("krope", t)])
        norm_squares(cx, lambda c: cq_sb[:, c, :], [("cq_sb", c) for c in range(3)], 3, sq2, "sq2")
        norm_rstd(cx, 3, sq2, "sq2", 7, rstd2, "rstd2", float(QL))
        cqt = cqn_t[t % 2]
        for c in range(3):
            p.add("vector", lambda e, c=c, cqt=cqt: e.scalar_tensor_tensor(out=cqt[:, c, :], in0=cq_sb[:, c, :], scalar=cx.vecs[:, qcol + c:qcol + c + 1], in1=rstd2,
                                                                          op0=ALU.mult, op1=ALU.mult),
                  reads=[("cq_sb", c), "rstd2", "vecs"], writes=[("cqn_t", t % 2)])
        p.add("sync", lambda e, cqt=cqt, t=t: e.dma_start(out=cx.cqn_d[t], in_=cqt.rearrange("p c n -> p (c n)")),
              reads=[("cqn_t", t % 2)], writes=[("cqn_d", t)], dsem=("cqn_w", t % 2))
        p.add("scalar", lambda e: e.activation(out=sq2[:, 3, :], in_=ckv_sb, func=AF.Square), reads=["ckv_sb"], writes=[("sq2", 3)])
        mm(cx, cx.ps[7][:, :], cx.ones[:, :], sq2[:, 3, :], True, True, reads=[("sq2", 3), "ones"], writes=[("ps", 7)])
        p.add("scalar", lambda e: e.activation(out=rstd3, in_=cx.ps[7][:, :], func=AF.Ln, bias=cx.epsc, scale=1.0 / KVL),
              reads=[("ps", 7), "epsc"], writes=["rstd2"])
        p.add("scalar", lambda e: e.activation(out=rstd3, in_=rstd3, func=AF.Exp, scale=-0.5), reads=["rstd2"], writes=["rstd2"])
        p.add("vector", lambda e, ts=ts: e.scalar_tensor_tensor(out=ckvn[:, ts], in0=ckv_sb, scalar=cx.vecs[:, kvcol:kvcol + 1], in1=rstd3,
                                                                op0=ALU.mult, op1=ALU.mult),
              reads=["ckv_sb", "rstd2", "vecs"], writes=[("ckvn", t)])
    ar.reset(m1)
    p.barrier()
    kn = ar.bf16(S)
    V = ar.bf16(NKC * 128).rearrange("p (c n) -> p c n", c=NKC)
    qn = [ar.bf16(TT) for _ in range(2)]
    qr = [ar.bf16(TT) for _ in range(2)]
    tabB = [ar.f32(2 * TT, 64).rearrange("p (c n) -> p c n", c=2) for _ in range(2)]
    NP = 4
    PT = [ar.bf16(TT) for _ in range(NP)]
    cqs = [ar.bf16(3 * TT).rearrange("p (c n) -> p c n", c=3) for _ in range(2)]
    wuq_s = [ar.bf16(3 * 256).rearrange("p (c n) -> p c n", c=3) for _ in range(2)]
    wukv = ar.bf16(NH * 256).rearrange("p (h n) -> p h n", h=NH)
    wo_s = [ar.bf16(D) for _ in range(2)]
    on = [ar.bf16(TT) for _ in range(2)]
    rden = [ar.f32(TT) for _ in range(2)]
    qt1 = ar.f32(TT, 64)
    qt2 = ar.f32(TT, 64)
    p.add("sync", lambda e: e.dma_start(out=wukv.rearrange("p h n -> p (h n)"), in_=cx.wbf["wukv"][j]),
          reads=[("wukv_d", j)], writes=["wukv_s"], dsem="wukv_s")
    for i in range(2):
        p.add("vector", lambda e, i=i: e.memset(qr[i][64:128, :], 0.0), writes=[("qr", i)])
    st = {"npt": 0, "nmisc": 0}
    pq = []
    pw = []

    def misc_bank():
        b = 6 + st["nmisc"] % 2
        st["nmisc"] += 1
        return b

    def q_loads(h, qt, qs):
        ts = slice(qt * TT, (qt + 1) * TT)
        p.add("sync", lambda e: e.dma_start(out=cqs[qs].rearrange("p c n -> p (c n)"), in_=cx.cqn_d[qt]),
              reads=[("cqn_d", qt)], writes=[("cqs", qs)], dsem=("cqs", qs))
        tb = tabB[qs]
        p.add("sync", lambda e: e.dma_start(out=tb[:, 0, :], in_=cx.tab[0][:, ts]),
              reads=[("tab", qt, 0)], writes=[("tabB", qs, 0)], dsem=("tabB", qs, 0))
        p.add("sync", lambda e: e.dma_start(out=tb[:, 1, :], in_=cx.tab[1][:, ts]),
              reads=[("tab", qt, 1)], writes=[("tabB", qs, 1)], dsem=("tabB", qs, 1))

    def q_tasks(h, qs):
        hs = h % 2
        tb = tabB[qs]

        def t_qn():
            b = misc_bank()
            for c in range(3):
                mm(cx, cx.ps[b][:, :], wuq_s[hs][:, c, 0:128], cqs[qs][:, c, :], c == 0, c == 2,
                   reads=[("wuq_s", hs), ("cqs", qs)], writes=[("ps", b)])
            p.add("vector", lambda e: e.tensor_copy(out=qn[qs], in_=cx.ps[b][:, :]),
                  reads=[("ps", b)], writes=[("qn", qs)])

        def t_qa():
            b = misc_bank()
            for c in range(3):
                mm(cx, cx.ps[b][0:64, :], wuq_s[hs][:, c, 128:192], cqs[qs][:, c, :], c == 0, c == 2,
                   reads=[("wuq_s", hs), ("cqs", qs)], writes=[("ps", b)])
            p.add("vector", lambda e: e.tensor_tensor(out=qt1, in0=cx.ps[b][0:64, :], in1=tb[:, 0, :], op=ALU.mult),
                  reads=[("ps", b), ("tabB", qs, 0)], writes=["qt1"])

        def t_qb():
            b = misc_bank()
            for c in range(3):
                mm(cx, cx.ps[b][0:64, :], wuq_s[hs][:, c, 192:256], cqs[qs][:, c, :], c == 0, c == 2,
                   reads=[("wuq_s", hs), ("cqs", qs)], writes=[("ps", b)])
            p.add("vector", lambda e: e.tensor_tensor(out=qt2, in0=cx.ps[b][0:64, :], in1=tb[:, 1, :], op=ALU.mult),
                  reads=[("ps", b), ("tabB", qs, 1)], writes=["qt2"])
            p.add("vector", lambda e: e.tensor_tensor(out=qr[qs][0:64, :], in0=qt1, in1=qt2, op=ALU.add),
                  reads=["qt1", "qt2"], writes=[("qr", qs)])

        return [t_qn, t_qa, t_qb]

    def wo_tasks(h, qt, osl):
        hs = h % 2
        ts = slice(qt * TT, (qt + 1) * TT)
        tasks = []
        for oc in range(DC):
            def t_wo(oc=oc):
                b = misc_bank()
                mm(cx, cx.ps[b][:, :], wo_s[hs][:, oc * 128:(oc + 1) * 128], on[osl], True, True,
                   reads=[("wo_s", hs), ("on", osl)], writes=[("ps", b)])
                rs = cx.res[:, oc, ts]
                p.add("vector", lambda e: e.tensor_tensor(out=rs, in0=rs, in1=cx.ps[b][:, :], op=ALU.add),
                      reads=[("ps", b), ("res", oc, qt)], writes=[("res", oc, qt)])
            tasks.append(t_wo)
        return tasks

    def flush(lst):
        while lst:
            lst.pop(0)()

    nq = 0
    for h in range(NH):
        hs = h % 2
        p.add("sync", lambda e, hs=hs, h=h: e.dma_start(out=wuq_s[hs].rearrange("p c n -> p (c n)"), in_=cx.wbf["wuq"][j][h]),
              reads=[("wuq_d", j)], writes=[("wuq_s", hs)], dsem=("wuq_s", hs))
        p.add("sync", lambda e, hs=hs, h=h: e.dma_start(out=wo_s[hs], in_=cx.wbf["wo"][j][:, h * D:(h + 1) * D]),
              reads=[("wo_d", j)], writes=[("wo_s", hs)], dsem=("wo_s", hs))
        q_loads(h, 0, nq % 2)
        kvb = [0, 1, 6, 7]
        for t in range(NT):
            ts = slice(t * TT, (t + 1) * TT)
            b = kvb[(2 * t) % 4]
            mm(cx, cx.ps[b][:, :], wukv[:, h, 0:128], ckvn[:, ts], True, True,
               reads=["wukv_s", ("ckvn", t)], writes=[("ps", b)])
            p.add("scalar", lambda e, ts=ts, b=b: e.activation(out=kn[:, ts], in_=cx.ps[b][:, :], func=AF.Copy),
                  reads=[("ps", b)], writes=[("kn", t)])
            b = kvb[(2 * t + 1) % 4]
            for i in range(4):
                kc = t * 4 + i
                mm(cx, cx.ps[b][:, i * 128:(i + 1) * 128], ckvn[:, kc * 128:(kc + 1) * 128], wukv[:, h, 128:256], True, True,
                   reads=["wukv_s", ("ckvn", t)], writes=[("ps", b)])
            p.add("vector", lambda e, t=t, b=b: e.tensor_copy(out=V[:, t * 4:(t + 1) * 4, :], in_=cx.ps[b][:, :].rearrange("p (c n) -> p c n", c=4)),
                  reads=[("ps", b)], writes=[("V", t)])
        flush(q_tasks(h, nq % 2))
        for qt in range(NT):
            qs = nq % 2
            osl = nq % 2
            nq += 1
            if qt + 1 < NT:
                q_loads(h, qt + 1, nq % 2)
                pq.extend(q_tasks(h, nq % 2))
            nk = 4 * (qt + 1)
            ob = 2 + osl
            db = 4 + osl

            def s_mm(kc):
                jd = kc - 4 * qt
                c0 = 128 * jd if jd > 0 else 0
                sb = kc % 2
                mm(cx, cx.ps[sb][:, c0:TT], kn[:, kc * 128:(kc + 1) * 128], qn[qs][:, c0:TT], True, False,
                   reads=[("kn", kc // 4), ("qn", qs)], writes=[("ps", sb)])
                mm(cx, cx.ps[sb][:, c0:TT], krope[:, kc * 128:(kc + 1) * 128], qr[qs][:, c0:TT], False, jd < 0,
                   reads=[("krope", kc // 4), ("qr", qs)], writes=[("ps", sb)])
                if jd >= 0:
                    mm(cx, cx.ps[sb][:, c0:c0 + 128], cx.ident[:, :], cx.tri[:, :], False, True,
                       reads=["ident", "tri"], writes=[("ps", sb)])

            for kc in range(min(2, nk)):
                s_mm(kc)
            for kc in range(nk):
                jd = kc - 4 * qt
                c0 = 128 * jd if jd > 0 else 0
                sb = kc % 2
                pt = PT[st["npt"] % NP]
                ptk = ("PT", st["npt"] % NP)
                st["npt"] += 1
                p.add("scalar", lambda e, pt=pt, sb=sb, c0=c0: e.activation(out=pt[:, c0:TT], in_=cx.ps[sb][:, c0:TT], func=AF.Exp, scale=sm_scale),
                      reads=[("ps", sb)], writes=[ptk])
                mm(cx, cx.ps[ob][:, c0:TT], V[:, kc, :], pt[:, c0:TT], kc == 0, kc == nk - 1,
                   reads=[("V", kc // 4), ptk], writes=[("ps", ob)])
                mm(cx, cx.ps[db][:, c0:TT], cx.ones[:, :], pt[:, c0:TT], kc == 0, kc == nk - 1,
                   reads=["ones", ptk], writes=[("ps", db)])
                if kc + 2 < nk:
                    s_mm(kc + 2)
                if kc == 1 and st.get("tail") is not None:
                    tl = st["tail"]
                    st["tail"] = None
                    tl()
                if pq:
                    pq.pop(0)()
                elif pw:
                    pw.pop(0)()
            flush(pq)
            while len(pw) > DC:
                pw.pop(0)()
            def tail(h=h, qt=qt, osl=osl, ob=ob, db=db):
                rd = rden[osl]
                p.add("scalar", lambda e: e.activation(out=rd, in_=cx.ps[db][:, :], func=AF.Ln),
                      reads=[("ps", db)], writes=[("rden", osl)])
                p.add("scalar", lambda e: e.activation(out=rd, in_=rd, func=AF.Exp, scale=-1.0),
                      reads=[("rden", osl)], writes=[("rden", osl)])
                p.add("vector", lambda e: e.tensor_tensor(out=on[osl], in0=cx.ps[ob][:, :], in1=rd, op=ALU.mult),
                      reads=[("ps", ob), ("rden", osl)], writes=[("on", osl)])
                pw.extend(wo_tasks(h, qt, osl))

            st["tail"] = tail
    if st.get("tail") is not None:
        st["tail"]()
        st["tail"] = None
    flush(pw)
    ar.reset(m0)


def final_phase(cx, do_norm):
    p = cx.p
    p.barrier()
    ar = cx.arena
    m0 = ar.mark()
    sq = ar.bf16(DC * TT).rearrange("p (c n) -> p c n", c=DC)
    rstd = ar.f32(TT)
    ob = [ar.f32(TT) for _ in range(4)]
    gcol = cx.vcol["norm_final"]
    n = 0
    outk = []
    for t in range(cx.NT):
        ts = slice(t * TT, (t + 1) * TT)
        if do_norm:
            norm_squares(cx, lambda c, ts=ts: cx.res[:, c, ts], [("res", c, t) for c in range(DC)], DC, sq, "sq")
            norm_rstd(cx, DC, sq, "sq", 6, rstd, "rstd", float(D))
        for c in range(DC):
            if do_norm:
                o = ob[n % 4]
                ok = ("ob", n % 4)
                n += 1
                p.add("vector", lambda e, c=c, o=o, ts=ts: e.scalar_tensor_tensor(out=o, in0=cx.res[:, c, ts], scalar=cx.vecs[:, gcol + c:gcol + c + 1], in1=rstd,
                                                                               op0=ALU.mult, op1=ALU.mult),
                      reads=[("res", c, t), "rstd", "vecs"], writes=[ok])
                p.add("sync", lambda e, c=c, o=o, ts=ts: e.dma_start(out=cx.outT[c * 128:(c + 1) * 128, ts], in_=o),
                      reads=[ok], writes=[("out", c, t)], dsem=("outw", n % 4))
            else:
                p.add("sync", lambda e, c=c, ts=ts: e.dma_start(out=cx.outT[c * 128:(c + 1) * 128, ts], in_=cx.res[:, c, ts]),
                      reads=[("res", c, t)], writes=[("out", c, t)], dsem=("outw", c % 4))
            outk.append(("out", c, t))
    p.add("sync", None, reads=outk)
    ar.reset(m0)


def vec_layout():
    col = 0
    vcol = {"norm_mix": [], "norm_ffn": [], "pool_scale": [], "q_norm": [], "kv_norm": []}
    for l in range(DEPTH):
        vcol["norm_mix"].append(col); col += DC
        vcol["norm_ffn"].append(col); col += DC
    for j in range(2):
        vcol["pool_scale"].append(col); col += DC
    for j in range(2):
        vcol["q_norm"].append(col); col += 3
        vcol["kv_norm"].append(col); col += 1
    vcol["norm_final"] = col; col += DC
    return vcol, col


def build_program(S, layers, do_final_norm):
    nc = bass.Bass("TRN2", target_bir_lowering=False)
    NT = S // TT
    vcol, NV = vec_layout()
    cx = Cx()
    cx.S = S
    cx.NT = NT
    cx.vcol = vcol
    xT = nc.dram_tensor("xT", [D, S], F32, kind="ExternalInput").ap()
    cx.pos = nc.dram_tensor("pos", [1, S], I32, kind="ExternalInput").ap()
    vecs_d = nc.dram_tensor("vecs", [128, NV], F32, kind="ExternalInput").ap()
    cst_d = nc.dram_tensor("cst", [128, 288], F32, kind="ExternalInput").ap()
    cx.outT = nc.dram_tensor("outT", [D, S], F32, kind="ExternalOutput").ap()
    pool_layers = [l for l in layers if l % 2 == 0]
    mla_layers = [l for l in layers if l % 2 == 1]
    w32 = {}
    cx.wbf = {"wgu": {}, "wd": {}, "wp": {}, "wdn": {}, "wuq": {}, "wukv": {}, "wo": {}}

    def wpair(name, shape):
        a = nc.dram_tensor(name, shape, F32, kind="ExternalInput").ap()
        b = nc.dram_tensor(name + "_bf", shape, BF16, kind="Internal").ap()
        return a, b

    for l in layers:
        w32[("wgu", l)], cx.wbf["wgu"][l] = wpair("wgu%d" % l, [FC, 128, DC * 2 * 128])
        w32[("wd", l)], cx.wbf["wd"][l] = wpair("wd%d" % l, [DC, 128, FC * 128])
    for l in pool_layers:
        j = l // 2
        w32[("wp", j)], cx.wbf["wp"][j] = wpair("wp%d" % j, [128, 2048])
    for l in mla_layers:
        j = l // 2
        w32[("wdn", j)], cx.wbf["wdn"][j] = wpair("wdn%d" % j, [128, DC * 640])
        w32[("wuq", j)], cx.wbf["wuq"][j] = wpair("wuq%d" % j, [NH, 128, 768])
        w32[("wukv", j)], cx.wbf["wukv"][j] = wpair("wukv%d" % j, [128, 2048])
        w32[("wo", j)], cx.wbf["wo"][j] = wpair("wo%d" % j, [128, NH * D])
    if mla_layers:
        cx.tab = [nc.dram_tensor("tab%d" % i, [64, S], F32, kind="Internal").ap() for i in range(2)]
        cx.cqn_d = nc.dram_tensor("cqn_d", [NT, 128, 3 * TT], BF16, kind="Internal").ap()

    with contextlib.ExitStack() as st:
        p = Prog(nc, st)
        cx.p = p
        res_t = st.enter_context(nc.sbuf_tensor("res", [128, DC * S], F32))
        cx.res = res_t[:, :].rearrange("p (c n) -> p c n", c=DC)
        cx.vecs = st.enter_context(nc.sbuf_tensor("vecs_sb", [128, NV], F32))[:, :]
        cst = st.enter_context(nc.sbuf_tensor("cst_sb", [128, 288], F32))[:, :]
        cx.ones = st.enter_context(nc.sbuf_tensor("ones_bf", [128, 128], BF16))[:, :]
        cx.tri = st.enter_context(nc.sbuf_tensor("tri_bf", [128, 128], BF16))[:, :]
        cx.ident = st.enter_context(nc.sbuf_tensor("ident_bf", [128, 128], BF16))[:, :]
        cx.epsc = st.enter_context(nc.sbuf_tensor("epsc", [128, 1], F32))[:, :]
        cx.invn = cst[:, 0:16]
        cx.cst64 = cst[0:64, 16:18]
        tri32 = cst[:, 32:160]
        remaining = nc.sbuf_bytes_remaining
        A = (remaining - 64) // 4
        cx.arena = Arena(st.enter_context(nc.sbuf_tensor("arena", [128, A], F32))[:, :], A)
        cx.ps = [st.enter_context(nc.psum_tensor("ps%d" % i, [128, TT], F32)) for i in range(8)]

        p.add("sync", lambda e: e.dma_start(out=cx.vecs, in_=vecs_d), writes=["vecs"], dsem="vecs")
        p.add("sync", lambda e: e.dma_start(out=cst, in_=cst_d), writes=["invn", "cst64", "tri32"], dsem="cst")
        p.add("vector", lambda e: e.memset(cx.ones, 1.0), writes=["ones"])
        p.add("vector", lambda e: e.memset(cx.epsc, EPS), writes=["epsc"])
        p.add("vector", lambda e: e.tensor_scalar(out=cx.tri, in0=tri32, scalar1=-1.0, scalar2=30000.0, op0=ALU.add, op1=ALU.mult),
              reads=["tri32"], writes=["tri"])
        p.add("vector", lambda e: e.tensor_copy(out=cx.ident, in_=cst[:, 160:288]), reads=["tri32"], writes=["ident"])
        cx.arena_full = A
        if mla_layers:
            R = S + 5 * TT
            top = Arena(cx.arena.ap[:, A - R:A], R)
            cx.arena.n = A - R
            posall = top.i32(S, 64)
            p.add("sync", lambda e: e.dma_start(out=posall, in_=cx.pos.partition_broadcast(64)), writes=["posall"], dsem="posall")
        res_loads = []
        for c in range(DC):
            res_loads.append(p.add("sync", lambda e, c=c: e.dma_start(out=cx.res[:, c, :], in_=xT[c * 128:(c + 1) * 128, :]),
                                   writes=[("res", c, t) for t in range(NT)], dsem=("resld", c)))
        def cvt_now(dst, src, key, after):
            p.add("gpsimd", lambda e: e.dma_start(out=dst, in_=src), writes=[key], dsem=("cv", key), nowaw=True, after=after)

        def layer_casts(l):
            j = l // 2
            out = []
            if l % 2 == 0:
                out.append((cx.wbf["wp"][j], w32[("wp", j)], ("wp_d", j)))
            else:
                out.append((cx.wbf["wdn"][j], w32[("wdn", j)], ("wdn_d", j)))
                out.append((cx.wbf["wukv"][j], w32[("wukv", j)], ("wukv_d", j)))
                for h in range(NH):
                    out.append((cx.wbf["wuq"][j][h], w32[("wuq", j)][h], ("wuq_d", j)))
                for h in range(NH):
                    out.append((cx.wbf["wo"][j][:, h * D:(h + 1) * D], w32[("wo", j)][:, h * D:(h + 1) * D], ("wo_d", j)))
            nmix = len(out)
            for fc in range(FC):
                out.append((cx.wbf["wgu"][l][fc], w32[("wgu", l)][fc], ("wgu_d", l, fc // 11)))
            for jj in range(DC):
                out.append((cx.wbf["wd"][l][jj], w32[("wd", l)][jj], ("wd_d", l)))
            return out, nmix

        tab_ops = []
        if mla_layers:
            tab_ops = rope_tables(cx, top, posall)
            if layers[0] % 2 == 1:
                p.barrier()
                cx.arena.n = A
        c0, nmix = layer_casts(layers[0])
        for i, (dst, src, key) in enumerate(c0):
            cvt_now(dst, src, key, list(res_loads) if i < nmix else list(res_loads) + tab_ops)
        cx.pending_casts = []
        cast_plan = {}
        for li in range(1, len(layers)):
            cl, _ = layer_casts(layers[li])
            cast_plan[layers[li - 1]] = [(lambda after, d=d, s_=s_, k=k: cvt_now(d, s_, k, after)) for (d, s_, k) in cl]
        for l in layers:
            if l % 2 == 0:
                pool_phase(cx, l, l // 2)
            else:
                mla_phase(cx, l, l // 2)
            cx.pending_casts = cast_plan.get(l, [])
            ffn_phase(cx, l)
            while cx.pending_casts:
                cx.pending_casts.pop(0)([])
        final_phase(cx, do_final_norm)
        p.emit()
    return nc


def host_consts():
    cst = np.zeros((128, 288), np.float32)
    cst[:, 0:16] = (1.0 / np.arange(1, 17, dtype=np.float32))[None, :]
    inv = (1.0 / (10000.0 ** (np.arange(0, ROPE, 2, dtype=np.float32) / ROPE))).astype(np.float32)
    cst[0:64, 16] = np.concatenate([inv, inv])
    cst[0:32, 17] = -1.0
    cst[32:64, 17] = 1.0
    k = np.arange(128)[:, None]
    q = np.arange(128)[None, :]
    cst[:, 32:160] = (k <= q).astype(np.float32)
    cst[:, 160:288] = np.eye(128, dtype=np.float32)
    return cst


def col_layout(v):
    n = v.shape[0] // 128
    return np.ascontiguousarray(v.reshape(n, 128).T)


def host_weights(inp, layers):
    vcol, NV = vec_layout()
    vecs = np.zeros((128, NV), np.float32)
    for l in range(DEPTH):
        vecs[:, vcol["norm_mix"][l]:vcol["norm_mix"][l] + DC] = col_layout(inp["norm_mix"][l])
        vecs[:, vcol["norm_ffn"][l]:vcol["norm_ffn"][l] + DC] = col_layout(inp["norm_ffn"][l])
    for j in range(2):
        vecs[:, vcol["pool_scale"][j]:vcol["pool_scale"][j] + DC] = col_layout(inp["pool_scale"][j])
        vecs[:, vcol["q_norm"][j]:vcol["q_norm"][j] + 3] = col_layout(inp["mla_q_norm"][j])
        vecs[:, vcol["kv_norm"][j]:vcol["kv_norm"][j] + 1] = col_layout(inp["mla_kv_norm"][j])
    vecs[:, vcol["norm_final"]:vcol["norm_final"] + DC] = col_layout(inp["norm_final"])
    w = {"vecs": vecs, "cst": host_consts()}
    for l in layers:
        wg = inp["ffn_w_gate"][l].reshape(DC, 128, FC, 128)
        wu = inp["ffn_w_up"][l].reshape(DC, 128, FC, 128)
        wgu = np.stack([wg, wu], axis=3)
        w["wgu%d" % l] = np.ascontiguousarray(wgu.transpose(2, 1, 0, 3, 4)).reshape(FC, 128, DC * 2 * 128)
        wd = inp["ffn_w_down"][l].reshape(FC, 128, DC, 128)
        w["wd%d" % l] = np.ascontiguousarray(wd.transpose(2, 1, 0, 3)).reshape(DC, 128, FC * 128)
        j = l // 2
        if l % 2 == 0:
            wp = inp["pool_w"][j].reshape(4, 2, 128, 256)
            w["wp%d" % j] = np.ascontiguousarray(wp.transpose(2, 0, 1, 3)).reshape(128, 2048)
        else:
            wdn = inp["mla_w_down"][j]
            kr = wdn[:, QL + KVL:]
            krs = np.concatenate([kr[:, 32:], kr[:, :32]], axis=1)
            wdn_aug = np.concatenate([wdn, krs], axis=1)
            w["wdn%d" % j] = np.ascontiguousarray(wdn_aug.reshape(DC, 128, 640).transpose(1, 0, 2)).reshape(128, DC * 640)
            wuq = inp["mla_w_uq"][j].reshape(3, 128, NH, 192)
            qr = wuq[..., 128:]
            qrs = np.concatenate([qr[..., 32:], qr[..., :32]], axis=-1)
            wuq_aug = np.concatenate([wuq, qrs], axis=-1)
            w["wuq%d" % j] = np.ascontiguousarray(wuq_aug.transpose(2, 1, 0, 3)).reshape(NH, 128, 768)
            w["wukv%d" % j] = np.ascontiguousarray(inp["mla_w_ukv"][j])
            wo = inp["mla_w_o"][j].reshape(NH, 128, D)
            w["wo%d" % j] = np.ascontiguousarray(wo.transpose(1, 0, 2)).reshape(128, NH * D)
    return w


_PROG_CACHE = {}


def run_layers(x_fm, positions, inp, layers, do_final_norm, n_cores):
    S = x_fm.shape[2]
    key = (S, tuple(layers), do_final_norm)
    if key not in _PROG_CACHE:
        _PROG_CACHE[key] = build_program(S, list(layers), do_final_norm)
    nc = _PROG_CACHE[key]
    w = host_weights(inp, layers)
    in_maps = []
    for b in range(n_cores):
        m = dict(w)
        m["xT"] = np.ascontiguousarray(x_fm[b])
        m["pos"] = np.ascontiguousarray(positions[b:b + 1].astype(np.int32))
        in_maps.append(m)
    res = run_bass_kernel_spmd(nc, in_maps, core_ids=list(range(n_cores)))
    return np.stack([r["outT"] for r in res.results], axis=0)


def kernel(**inputs):
    inp = {k: np.asarray(v) for k, v in inputs.items()}
    x = inp["x"].astype(np.float32, copy=False)
    B = x.shape[0]
    x_fm = np.ascontiguousarray(x.transpose(0, 2, 1))
    out_fm = run_layers(x_fm, inp["positions"], inp, [0, 1, 2, 3], True, B)
    return np.ascontiguousarray(out_fm.transpose(0, 2, 1))
```

```python
import contextlib
import math
import numpy as np
import concourse.bass as bass
import concourse.mybir as mybir
from concourse.bass_utils import run_bass_kernel_spmd

F32 = mybir.dt.float32
BF16 = mybir.dt.bfloat16
I32 = mybir.dt.int32
ALU = mybir.AluOpType
AF = mybir.ActivationFunctionType

D = 1024
DC = 8
DFF = 2816
FC = 22
NH = 8
QL = 384
KVL = 128
ROPE = 64
TT = 512
EPS = 1e-6
DEPTH = 4
EPOCH = 20000
N_CORES = 8


class Op:
    __slots__ = ("eng", "fn", "deps", "needs_inc", "sig", "is_dma", "dsem")


class Prog:
    ENGS = ["sync", "tensor", "vector", "scalar", "gpsimd"]

    def __init__(self, nc, stack):
        self.nc = nc
        self.stack = stack
        self.ops = {e: [] for e in self.ENGS}
        self.lastw = {}
        self.readers = {}
        self.dma_sems = {}
        self.eng_sems = {}
        self.pending = {}

    def barrier(self):
        deps = []
        for e in self.ENGS:
            for op in reversed(self.ops[e]):
                if not op.is_dma and op.fn is not None:
                    deps.append(op)
                    break
        seen = set()
        for e in self.ENGS:
            for op in reversed(self.ops[e]):
                if op.is_dma and op.dsem not in seen:
                    seen.add(op.dsem)
                    if not (isinstance(op.dsem, tuple) and op.dsem[0] == "cv"):
                        deps.append(op)
        self.pending = {e: list(deps) for e in self.ENGS}

    def _new_sem(self, name):
        return self.stack.enter_context(self.nc.semaphore(name))

    def add(self, eng, fn, reads=(), writes=(), dsem=None, nowaw=False, after=()):
        op = Op()
        op.eng = eng
        op.fn = fn
        op.deps = set()
        op.needs_inc = False
        op.sig = None
        op.is_dma = dsem is not None
        op.dsem = dsem
        for k in reads:
            w = self.lastw.get(k)
            if w is not None:
                op.deps.add(w)
        for k in writes:
            w = self.lastw.get(k)
            if w is not None and (w.is_dma or w.eng != eng) and not nowaw:
                op.deps.add(w)
            for r in self.readers.get(k, ()):
                if r is not op and (r.is_dma or r.eng != eng):
                    op.deps.add(r)
        if eng == "tensor":
            op.deps = {d for d in op.deps if d.is_dma or d.eng != "tensor"}
        op.deps.update(after)
        pend = self.pending.pop(eng, None)
        if pend:
            op.deps.update(d for d in pend if d.is_dma or d.eng != eng or eng != "tensor")
        for d in op.deps:
            d.needs_inc = True
        for k in reads:
            self.readers.setdefault(k, []).append(op)
        for k in writes:
            self.lastw[k] = op
            self.readers[k] = []
        if op.is_dma:
            ent = self.dma_sems.get(dsem)
            if ent is None:
                ent = [self._new_sem("d%d_%s" % (len(self.dma_sems), "".join(ch for ch in str(dsem) if ch.isalnum()))), 0]
                self.dma_sems[dsem] = ent
            ent[1] += 16
            op.sig = (ent[0], ent[1])
        self.ops[eng].append(op)
        return op

    def _assign(self):
        for e in self.ENGS:
            cnt = 0
            for op in self.ops[e]:
                if op.is_dma or not op.needs_inc:
                    continue
                ep = cnt // EPOCH
                key = (e, ep)
                if key not in self.eng_sems:
                    self.eng_sems[key] = self._new_sem("c_%s_%d" % (e, ep))
                cnt += 1
                op.sig = (self.eng_sems[key], cnt - ep * EPOCH)

    def emit(self):
        self._assign()
        nc = self.nc
        with nc.Block() as block:
            for e in self.ENGS:
                ops = self.ops[e]

                def body(eng, ops=ops):
                    waited = {}
                    for op in ops:
                        need = {}
                        for d in op.deps:
                            sem, val = d.sig
                            if need.get(id(sem), (None, 0))[1] < val:
                                need[id(sem)] = (sem, val)
                        for sid, (sem, val) in need.items():
                            if waited.get(sid, 0) < val:
                                eng.wait_ge(sem, val)
                                waited[sid] = val
                        if op.fn is None:
                            continue
                        ins = op.fn(eng)
                        if op.is_dma:
                            ins.then_inc(op.sig[0], 16)
                        elif op.needs_inc:
                            ins.then_inc(op.sig[0], 1)

                getattr(block, e)(body)


class Arena:
    def __init__(self, ap, nwords):
        self.ap = ap
        self.n = nwords
        self.off = 0

    def mark(self):
        return self.off

    def reset(self, m):
        self.off = m

    def f32(self, ncols, parts=128):
        a = self.ap[0:parts, self.off:self.off + ncols]
        self.off += ncols
        assert self.off <= self.n, ("arena overflow", self.off, self.n)
        return a

    def bf16(self, ncols, parts=128):
        w = (ncols + 1) // 2
        a = self.ap[0:parts, self.off:self.off + w].bitcast(BF16)
        self.off += w
        assert self.off <= self.n, ("arena overflow", self.off, self.n)
        return a

    def i32(self, ncols, parts=128):
        a = self.ap[0:parts, self.off:self.off + ncols].bitcast(I32)
        self.off += ncols
        assert self.off <= self.n
        return a


class Cx:
    pass


def mm(cx, out, lhsT, rhs, start, stop, reads, writes):
    return cx.p.add("tensor", lambda e: e.matmul(out, lhsT, rhs, start=start, stop=stop), reads=reads, writes=writes)


def norm_squares(cx, src_fn, src_keys, nch, sq, sqkey):
    p = cx.p
    for c in range(nch):
        p.add("scalar", lambda e, c=c: e.activation(out=sq[:, c, :], in_=src_fn(c), func=AF.Square),
              reads=[src_keys[c]], writes=[(sqkey, c)])


def norm_rstd(cx, nch, sq, sqkey, bank, rstd, rstdkey, dim):
    p = cx.p
    for c in range(nch):
        mm(cx, cx.ps[bank][:, :], cx.ones[:, :], sq[:, c, :], c == 0, c == nch - 1,
           reads=[(sqkey, c), "ones"], writes=[("ps", bank)])
    p.add("scalar", lambda e: e.activation(out=rstd, in_=cx.ps[bank][:, :], func=AF.Ln, bias=cx.epsc, scale=1.0 / dim),
          reads=[("ps", bank), "epsc"], writes=[rstdkey])
    p.add("scalar", lambda e: e.activation(out=rstd, in_=rstd, func=AF.Exp, scale=-0.5), reads=[rstdkey], writes=[rstdkey])


def norm_apply_res(cx, t, gcol0, rstd, rstdkey, hT, hkey):
    p = cx.p
    for c in range(DC):
        p.add("vector",
              lambda e, c=c: e.scalar_tensor_tensor(out=hT[:, c, :], in0=cx.res[:, c, t * TT:(t + 1) * TT],
                                                    scalar=cx.vecs[:, gcol0 + c:gcol0 + c + 1], in1=rstd,
                                                    op0=ALU.mult, op1=ALU.mult),
              reads=[("res", c, t), rstdkey, "vecs"], writes=[(hkey, c)])


def ffn_phase(cx, l):
    p = cx.p
    p.barrier()
    cx.arena.n = cx.arena_full
    ar = cx.arena
    m0 = ar.mark()
    NT = cx.NT
    sq = ar.bf16(DC * TT).rearrange("p (c n) -> p c n", c=DC)
    rstd = ar.f32(TT)
    hT = ar.bf16(DC * TT).rearrange("p (c n) -> p c n", c=DC)
    gT = ar.bf16(FC * TT).rearrange("p (c n) -> p c n", c=FC)
    stmp = [ar.f32(TT) for _ in range(2)]
    NW1, NW2 = 4, 3
    wgu_s = [ar.bf16(DC * 2 * 128).rearrange("p (c g n) -> p c g n", c=DC, g=2) for _ in range(NW1)]
    wd_s = [ar.bf16(FC * 128).rearrange("p (c n) -> p c n", c=FC) for _ in range(NW2)]
    gcol = cx.vcol["norm_ffn"][l]
    wgu_d = cx.wbf["wgu"][l]
    wd_d = cx.wbf["wd"][l]
    uid = ("ffn", l)

    def squares(t):
        norm_squares(cx, lambda c, t=t: cx.res[:, c, t * TT:(t + 1) * TT], [("res", c, t) for c in range(DC)], DC, sq, "sq")

    def rest(t):
        norm_rstd(cx, DC, sq, "sq", 6, rstd, "rstd", float(D))
        norm_apply_res(cx, t, gcol, rstd, "rstd", hT, "hT")

    squares(0)
    rest(0)
    n1 = 0
    n2 = 0
    for t in range(NT):
        for fc in range(FC):
            s = n1 % NW1
            n1 += 1
            p.add("sync", lambda e, s=s, fc=fc: e.dma_start(out=wgu_s[s].rearrange("p c g n -> p (c g n)"), in_=wgu_d[fc]),
                  reads=[("wgu_d", l, fc // 11)], writes=[("wgu_s", s)], dsem=("wgu_s", s))
            bg = fc % 2
            bu = 2 + fc % 2
            for dc in range(DC):
                mop = mm(cx, cx.ps[bg][:, :], wgu_s[s][:, dc, 0, :], hT[:, dc, :], dc == 0, dc == DC - 1,
                         reads=[("wgu_s", s), ("hT", dc)], writes=[("ps", bg)])
                if dc == 0 and cx.pending_casts and (t * FC + fc) % 3 == 0:
                    cx.pending_casts.pop(0)([mop])
            for dc in range(DC):
                mm(cx, cx.ps[bu][:, :], wgu_s[s][:, dc, 1, :], hT[:, dc, :], dc == 0, dc == DC - 1,
                   reads=[("wgu_s", s), ("hT", dc)], writes=[("ps", bu)])
            st = stmp[fc % 2]
            p.add("scalar", lambda e, st=st, bg=bg: e.activation(out=st, in_=cx.ps[bg][:, :], func=AF.Silu),
                  reads=[("ps", bg)], writes=[("stmp", fc % 2)])
            p.add("vector", lambda e, st=st, bu=bu, fc=fc: e.tensor_tensor(out=gT[:, fc, :], in0=st, in1=cx.ps[bu][:, :], op=ALU.mult),
                  reads=[("stmp", fc % 2), ("ps", bu)], writes=[("gT", fc)])
            if fc == 11 and t + 1 < NT:
                squares(t + 1)
        if t + 1 < NT:
            rest(t + 1)
        for j in range(DC):
            s = n2 % NW2
            n2 += 1
            p.add("sync", lambda e, s=s, j=j: e.dma_start(out=wd_s[s].rearrange("p c n -> p (c n)"), in_=wd_d[j]),
                  reads=[("wd_d", l)], writes=[("wd_s", s)], dsem=("wd_s", s))
            b = 4 + j % 2
            for fc in range(FC):
                mm(cx, cx.ps[b][:, :], wd_s[s][:, fc, :], gT[:, fc, :], fc == 0, fc == FC - 1,
                   reads=[("wd_s", s), ("gT", fc)], writes=[("ps", b)])
            rs = cx.res[:, j, t * TT:(t + 1) * TT]
            p.add("vector", lambda e, rs=rs, b=b: e.tensor_tensor(out=rs, in0=rs, in1=cx.ps[b][:, :], op=ALU.add),
                  reads=[("ps", b), ("res", j, t)], writes=[("res", j, t)])
    ar.reset(m0)


POOL_WINDOWS = (2, 4, 8, 16)
HALO = 16


def pool_phase(cx, l, j):
    p = cx.p
    p.barrier()
    ar = cx.arena
    m0 = ar.mark()
    NT = cx.NT
    W = HALO + TT
    sq = ar.bf16(DC * TT).rearrange("p (c n) -> p c n", c=DC)
    rstd = ar.f32(TT)
    hb = ar.f32(DC * W).rearrange("p (c n) -> p c n", c=DC)
    NA = 4
    ab = [ar.f32(W) for _ in range(NA)]
    pT = ar.bf16(DC * TT).rearrange("p (c n) -> p c n", c=DC)
    wp = ar.bf16(4 * 2 * 256).rearrange("p (g c n) -> p g c n", g=4, c=2)
    ftmp = ar.f32(HALO)
    gcol = cx.vcol["norm_mix"][l]
    scol = cx.vcol["pool_scale"][j]
    p.add("sync", lambda e: e.dma_start(out=wp.rearrange("p g c n -> p (g c n)"), in_=cx.wbf["wp"][j]),
          reads=[("wp_d", j)], writes=["wp_s"], dsem="wp_s")
    for c in range(DC):
        p.add("vector", lambda e, c=c: e.memset(hb[:, c, 0:HALO], 0.0), writes=[("hb", c)])
    na = 0
    for t in range(NT):
        norm_squares(cx, lambda c, t=t: cx.res[:, c, t * TT:(t + 1) * TT], [("res", c, t) for c in range(DC)], DC, sq, "sq")
        norm_rstd(cx, DC, sq, "sq", 6, rstd, "rstd", float(D))
        for c in range(DC):
            p.add("vector",
                  lambda e, c=c, t=t: e.scalar_tensor_tensor(out=hb[:, c, HALO:W], in0=cx.res[:, c, t * TT:(t + 1) * TT],
                                                        scalar=cx.vecs[:, gcol + c:gcol + c + 1], in1=rstd,
                                                        op0=ALU.mult, op1=ALU.mult),
                  reads=[("res", c, t), "rstd", "vecs"], writes=[("hb", c)])
        for c in range(DC):
            g = c // 2
            win = POOL_WINDOWS[g]
            nlev = g + 1
            src = hb[:, c, :]
            srck = ("hb", c)
            lo = 0
            for lev in range(nlev):
                sh = 1 << lev
                lo = lo + sh
                a = ab[na % NA]
                ak = ("ab", na % NA)
                na += 1
                p.add("vector", lambda e, a=a, src=src, lo=lo, sh=sh: e.tensor_tensor(out=a[:, lo:W], in0=src[:, lo:W], in1=src[:, lo - sh:W - sh], op=ALU.add),
                      reads=[srck], writes=[ak])
                src = a
                srck = ak
            p.add("vector", lambda e, src=src, c=c, win=win: e.scalar_tensor_tensor(out=pT[:, c, :], in0=src[:, HALO:W], scalar=1.0 / win, in1=hb[:, c, HALO:W],
                                                                                op0=ALU.mult, op1=ALU.subtract),
                  reads=[srck, ("hb", c)], writes=[("pT", c)])
            if t == 0:
                n = win - 1
                p.add("vector", lambda e, src=src, n=n: e.tensor_tensor(out=ftmp[:, 0:n], in0=src[:, HALO:HALO + n], in1=cx.invn[:, 0:n], op=ALU.mult),
                      reads=[srck, "invn"], writes=["ftmp"])
                p.add("vector", lambda e, c=c, n=n: e.tensor_tensor(out=pT[:, c, 0:n], in0=ftmp[:, 0:n], in1=hb[:, c, HALO:HALO + n], op=ALU.subtract),
                      reads=["ftmp", ("hb", c)], writes=[("pT", c)])
        if t + 1 < NT:
            for c in range(DC):
                p.add("vector", lambda e, c=c: e.tensor_copy(out=hb[:, c, 0:HALO], in_=hb[:, c, TT:W]),
                      reads=[("hb", c)], writes=[("hb", c)])
        for oc in range(DC):
            g = oc // 2
            o2 = oc % 2
            b = oc % 4
            for cc in range(2):
                mm(cx, cx.ps[b][:, :], wp[:, g, cc, o2 * 128:(o2 + 1) * 128], pT[:, 2 * g + cc, :], cc == 0, cc == 1,
                   reads=["wp_s", ("pT", 2 * g + cc)], writes=[("ps", b)])
            rs = cx.res[:, oc, t * TT:(t + 1) * TT]
            p.add("vector", lambda e, rs=rs, b=b, oc=oc: e.scalar_tensor_tensor(out=rs, in0=cx.ps[b][:, :], scalar=cx.vecs[:, scol + oc:scol + oc + 1], in1=rs,
                                                                                op0=ALU.mult, op1=ALU.add),
                  reads=[("ps", b), ("res", oc, t), "vecs"], writes=[("res", oc, t)])
    ar.reset(m0)


def rope_tables(cx, top, posall):
    p = cx.p
    ar = top
    S = cx.S
    C1 = 6.28125
    C2 = 2 * math.pi - 6.28125
    ang, kf, sn, cs = ar.f32(TT, 64), ar.f32(TT, 64), ar.f32(TT, 64), ar.f32(TT, 64)
    ki = ar.i32(TT, 64)
    tab_ops = []
    for t in range(cx.NT):
        posi = posall[:, t * TT:(t + 1) * TT]
        u = ("rt", 0)
        p.add("vector", lambda e, ang=ang, posi=posi: e.tensor_copy(out=ang, in_=posi), reads=["posall"], writes=[(u, "ang")])
        p.add("vector", lambda e, ang=ang: e.tensor_scalar(out=ang, in0=ang, scalar1=cx.cst64[:, 0:1], scalar2=None, op0=ALU.mult),
              reads=[(u, "ang"), "cst64"], writes=[(u, "ang")])
        p.add("vector", lambda e, ang=ang, kf=kf: e.tensor_single_scalar(out=kf, in_=ang, scalar=1.0 / (2 * math.pi), op=ALU.mult),
              reads=[(u, "ang")], writes=[(u, "kf")])
        p.add("vector", lambda e, ki=ki, kf=kf: e.tensor_copy(out=ki, in_=kf), reads=[(u, "kf")], writes=[(u, "ki")])
        p.add("vector", lambda e, ki=ki, kf=kf: e.tensor_copy(out=kf, in_=ki), reads=[(u, "ki")], writes=[(u, "kf")])
        p.add("vector", lambda e, ang=ang, kf=kf: e.scalar_tensor_tensor(out=ang, in0=kf, scalar=-C1, in1=ang, op0=ALU.mult, op1=ALU.add),
              reads=[(u, "kf"), (u, "ang")], writes=[(u, "ang")])
        p.add("vector", lambda e, ang=ang, kf=kf: e.scalar_tensor_tensor(out=ang, in0=kf, scalar=-C2, in1=ang, op0=ALU.mult, op1=ALU.add),
              reads=[(u, "kf"), (u, "ang")], writes=[(u, "ang")])
        p.add("vector", lambda e, ang=ang, kf=kf: e.tensor_scalar(out=kf, in0=ang, scalar1=math.pi, scalar2=-2 * math.pi, op0=ALU.is_gt, op1=ALU.mult),
              reads=[(u, "ang")], writes=[(u, "kf")])
        p.add("vector", lambda e, ang=ang, kf=kf: e.tensor_tensor(out=ang, in0=ang, in1=kf, op=ALU.add),
              reads=[(u, "ang"), (u, "kf")], writes=[(u, "ang")])
        p.add("scalar", lambda e, ang=ang, sn=sn: e.activation(out=sn, in_=ang, func=AF.Sin), reads=[(u, "ang")], writes=[(u, "sn")])
        p.add("vector", lambda e, sn=sn: e.tensor_scalar(out=sn, in0=sn, scalar1=cx.cst64[:, 1:2], scalar2=None, op0=ALU.mult),
              reads=[(u, "sn"), "cst64"], writes=[(u, "sn")])
        p.add("vector", lambda e, ang=ang: e.tensor_single_scalar(out=ang, in_=ang, scalar=math.pi / 2, op=ALU.add),
              reads=[(u, "ang"), (u, "sn")], writes=[(u, "ang")])
        p.add("vector", lambda e, ang=ang, kf=kf: e.tensor_scalar(out=kf, in0=ang, scalar1=math.pi, scalar2=-2 * math.pi, op0=ALU.is_gt, op1=ALU.mult),
              reads=[(u, "ang")], writes=[(u, "kf")])
        p.add("vector", lambda e, ang=ang, kf=kf: e.tensor_tensor(out=ang, in0=ang, in1=kf, op=ALU.add),
              reads=[(u, "ang"), (u, "kf")], writes=[(u, "ang")])
        p.add("scalar", lambda e, ang=ang, cs=cs: e.activation(out=cs, in_=ang, func=AF.Sin), reads=[(u, "ang")], writes=[(u, "cs")])
        tab_ops.append(p.add("sync", lambda e, cs=cs, t=t: e.dma_start(out=cx.tab[0][:, t * TT:(t + 1) * TT], in_=cs),
                             reads=[(u, "cs")], writes=[("tab", t, 0)], dsem=("tabw", t, 0)))
        tab_ops.append(p.add("sync", lambda e, sn=sn, t=t: e.dma_start(out=cx.tab[1][:, t * TT:(t + 1) * TT], in_=sn),
                             reads=[(u, "sn")], writes=[("tab", t, 1)], dsem=("tabw", t, 1)))
    return tab_ops


def mla_phase(cx, l, j):
    p = cx.p
    p.barrier()
    cx.arena.n = cx.arena_full
    ar = cx.arena
    m0 = ar.mark()
    NT = cx.NT
    S = cx.S
    NKC = S // 128
    sm_scale = float((128 + ROPE) ** -0.5)
    ckvn = ar.bf16(S)
    krope = ar.bf16(S)
    m1 = ar.mark()
    sq = ar.bf16(DC * TT).rearrange("p (c n) -> p c n", c=DC)
    rstd = ar.f32(TT)
    rstd2 = ar.f32(TT)
    hT = ar.bf16(DC * TT).rearrange("p (c n) -> p c n", c=DC)
    cq_sb = ar.f32(3 * TT).rearrange("p (c n) -> p c n", c=3)
    ckv_sb = ar.f32(TT)
    cqn_t = [ar.bf16(3 * TT).rearrange("p (c n) -> p c n", c=3) for _ in range(2)]
    tabA = [ar.f32(2 * TT, 64).rearrange("p (c n) -> p c n", c=2) for _ in range(2)]
    rt1 = ar.f32(TT, 64)
    rt2 = ar.f32(TT, 64)
    wdn = ar.bf16(DC * 640).rearrange("p (c n) -> p c n", c=DC)
    gcol = cx.vcol["norm_mix"][l]
    qcol = cx.vcol["q_norm"][j]
    kvcol = cx.vcol["kv_norm"][j]
    p.add("vector", lambda e: e.memset(krope[64:128, :], 0.0), writes=[("krope", t) for t in range(NT)])
    p.add("sync", lambda e: e.dma_start(out=wdn.rearrange("p c n -> p (c n)"), in_=cx.wbf["wdn"][j]),
          reads=[("wdn_d", j)], writes=["wdn_s"], dsem="wdn_s")
    sq2 = ar.bf16(4 * TT).rearrange("p (c n) -> p c n", c=4)
    rstd3 = rstd2

    def stage1(t):
        ts = slice(t * TT, (t + 1) * TT)
        norm_squares(cx, lambda c, ts=ts: cx.res[:, c, ts], [("res", c, t) for c in range(DC)], DC, sq, "sq")
        norm_rstd(cx, DC, sq, "sq", 6, rstd, "rstd", float(D))
        norm_apply_res(cx, t, gcol, rstd, "rstd", hT, "hT")

    stage1(0)
    for t in range(NT):
        ts = slice(t * TT, (t + 1) * TT)
        tb = tabA[t % 2]
        p.add("sync", lambda e, tb=tb, ts=ts: e.dma_start(out=tb[:, 0, :], in_=cx.tab[0][:, ts]),
              reads=[("tab", t, 0), ("tab", t, 1)], writes=[("tabA", t % 2)], dsem=("tabA", t % 2))
        p.add("sync", lambda e, tb=tb, ts=ts: e.dma_start(out=tb[:, 1, :], in_=cx.tab[1][:, ts]),
              reads=[("tab", t, 0), ("tab", t, 1)], writes=[("tabA", t % 2)], dsem=("tabA", t % 2))
        outs = [(0, 0, 128), (1, 128, 128), (2, 256, 128), (3, 384, 128), (4, 512, 64), (5, 576, 64)]
        for (b, c0, m) in outs:
            for dc in range(DC):
                mm(cx, cx.ps[b][0:m, :], wdn[:, dc, c0:c0 + m], hT[:, dc, :], dc == 0, dc == DC - 1,
                   reads=["wdn_s", ("hT", dc)], writes=[("ps", b)])
        if t + 1 < NT:
            stage1(t + 1)
        for c in range(3):
            p.add("scalar", lambda e, c=c: e.activation(out=cq_sb[:, c, :], in_=cx.ps[c][:, :], func=AF.Copy),
                  reads=[("ps", c)], writes=[("cq_sb", c)])
        p.add("scalar", lambda e: e.activation(out=ckv_sb, in_=cx.ps[3][:, :], func=AF.Copy),
              reads=[("ps", 3)], writes=["ckv_sb"])
        p.add("vector", lambda e, tb=tb: e.tensor_tensor(out=rt1, in0=cx.ps[4][0:64, :], in1=tb[:, 0, :], op=ALU.mult),
              reads=[("ps", 4), ("tabA", t % 2)], writes=["rt1"])
        p.add("vector", lambda e, tb=tb: e.tensor_tensor(out=rt2, in0=cx.ps[5][0:64, :], in1=tb[:, 1, :], op=ALU.mult),
              reads=[("ps", 5), ("tabA", t % 2)], writes=["rt2"])
        p.add("vector", lambda e, ts=ts: e.tensor_tensor(out=krope[0:64, ts], in0=rt1, in1=rt2, op=ALU.add),
              reads=["rt1", "rt2"], writes=[("krope", t)])
        norm_squares(cx, lambda c: cq_sb[:, c, :], [("cq_sb", c) for c in range(3)], 3, sq2, "sq2")
        norm_rstd(cx, 3, sq2, "sq2", 7, rstd2, "rstd2", float(QL))
        cqt = cqn_t[t % 2]
        for c in range(3):
            p.add("vector", lambda e, c=c, cqt=cqt: e.scalar_tensor_tensor(out=cqt[:, c, :], in0=cq_sb[:, c, :], scalar=cx.vecs[:, qcol + c:qcol + c + 1], in1=rstd2,
                                                                          op0=ALU.mult, op1=ALU.mult),
                  reads=[("cq_sb", c), "rstd2", "vecs"], writes=[("cqn_t", t % 2)])
        p.add("sync", lambda e, cqt=cqt, t=t: e.dma_start(out=cx.cqn_d[t], in_=cqt.rearrange("p c n -> p (c n)")),
              reads=[("cqn_t", t % 2)], writes=[("cqn_d", t)], dsem=("cqn_w", t % 2))
        p.add("scalar", lambda e: e.activation(out=sq2[:, 3, :], in_=ckv_sb, func=AF.Square), reads=["ckv_sb"], writes=[("sq2", 3)])
        mm(cx, cx.ps[7][:, :], cx.ones[:, :], sq2[:, 3, :], True, True, reads=[("sq2", 3), "ones"], writes=[("ps", 7)])
        p.add("scalar", lambda e: e.activation(out=rstd3, in_=cx.ps[7][:, :], func=AF.Ln, bias=cx.epsc, scale=1.0 / KVL),
              reads=[("ps", 7), "epsc"], writes=["rstd2"])
        p.add("scalar", lambda e: e.activation(out=rstd3, in_=rstd3, func=AF.Exp, scale=-0.5), reads=["rstd2"], writes=["rstd2"])
        p.add("vector", lambda e, ts=ts: e.scalar_tensor_tensor(out=ckvn[:, ts], in0=ckv_sb, scalar=cx.vecs[:, kvcol:kvcol + 1], in1=rstd3,
                                                                op0=ALU.mult, op1=ALU.mult),
              reads=["ckv_sb", "rstd2", "vecs"], writes=[("ckvn", t)])
    ar.reset(m1)
    p.barrier()
    kn = ar.bf16(S)
    V = ar.bf16(NKC * 128).rearrange("p (c n) -> p c n", c=NKC)
    qn = [ar.bf16(TT) for _ in range(2)]
    qr = [ar.bf16(TT) for _ in range(2)]
    tabB = [ar.f32(2 * TT, 64).rearrange("p (c n) -> p c n", c=2) for _ in range(2)]
    NP = 4
    PT = [ar.bf16(TT) for _ in range(NP)]
    cqs = [ar.bf16(3 * TT).rearrange("p (c n) -> p c n", c=3) for _ in range(2)]
    wuq_s = [ar.bf16(3 * 256).rearrange("p (c n) -> p c n", c=3) for _ in range(2)]
    wukv = ar.bf16(NH * 256).rearrange("p (h n) -> p h n", h=NH)
    wo_s = [ar.bf16(D) for _ in range(2)]
    on = [ar.bf16(TT) for _ in range(2)]
    rden = [ar.f32(TT) for _ in range(2)]
    qt1 = ar.f32(TT, 64)
    qt2 = ar.f32(TT, 64)
    p.add("sync", lambda e: e.dma_start(out=wukv.rearrange("p h n -> p (h n)"), in_=cx.wbf["wukv"][j]),
          reads=[("wukv_d", j)], writes=["wukv_s"], dsem="wukv_s")
    for i in range(2):
        p.add("vector", lambda e, i=i: e.memset(qr[i][64:128, :], 0.0), writes=[("qr", i)])
    st = {"npt": 0, "nmisc": 0}
    pq = []
    pw = []

    def misc_bank():
        b = 6 + st["nmisc"] % 2
        st["nmisc"] += 1
        return b

    def q_loads(h, qt, qs):
        ts = slice(qt * TT, (qt + 1) * TT)
        p.add("sync", lambda e: e.dma_start(out=cqs[qs].rearrange("p c n -> p (c n)"), in_=cx.cqn_d[qt]),
              reads=[("cqn_d", qt)], writes=[("cqs", qs)], dsem=("cqs", qs))
        tb = tabB[qs]
        p.add("sync", lambda e: e.dma_start(out=tb[:, 0, :], in_=cx.tab[0][:, ts]),
              reads=[("tab", qt, 0)], writes=[("tabB", qs, 0)], dsem=("tabB", qs, 0))
        p.add("sync", lambda e: e.dma_start(out=tb[:, 1, :], in_=cx.tab[1][:, ts]),
              reads=[("tab", qt, 1)], writes=[("tabB", qs, 1)], dsem=("tabB", qs, 1))

    def q_tasks(h, qs):
        hs = h % 2
        tb = tabB[qs]

        def t_qn():
            b = misc_bank()
            for c in range(3):
                mm(cx, cx.ps[b][:, :], wuq_s[hs][:, c, 0:128], cqs[qs][:, c, :], c == 0, c == 2,
                   reads=[("wuq_s", hs), ("cqs", qs)], writes=[("ps", b)])
            p.add("vector", lambda e: e.tensor_copy(out=qn[qs], in_=cx.ps[b][:, :]),
                  reads=[("ps", b)], writes=[("qn", qs)])

        def t_qa():
            b = misc_bank()
            for c in range(3):
                mm(cx, cx.ps[b][0:64, :], wuq_s[hs][:, c, 128:192], cqs[qs][:, c, :], c == 0, c == 2,
                   reads=[("wuq_s", hs), ("cqs", qs)], writes=[("ps", b)])
            p.add("vector", lambda e: e.tensor_tensor(out=qt1, in0=cx.ps[b][0:64, :], in1=tb[:, 0, :], op=ALU.mult),
                  reads=[("ps", b), ("tabB", qs, 0)], writes=["qt1"])

        def t_qb():
            b = misc_bank()
            for c in range(3):
                mm(cx, cx.ps[b][0:64, :], wuq_s[hs][:, c, 192:256], cqs[qs][:, c, :], c == 0, c == 2,
                   reads=[("wuq_s", hs), ("cqs", qs)], writes=[("ps", b)])
            p.add("vector", lambda e: e.tensor_tensor(out=qt2, in0=cx.ps[b][0:64, :], in1=tb[:, 1, :], op=ALU.mult),
                  reads=[("ps", b), ("tabB", qs, 1)], writes=["qt2"])
            p.add("vector", lambda e: e.tensor_tensor(out=qr[qs][0:64, :], in0=qt1, in1=qt2, op=ALU.add),
                  reads=["qt1", "qt2"], writes=[("qr", qs)])

        return [t_qn, t_qa, t_qb]

    def wo_tasks(h, qt, osl):
        hs = h % 2
        ts = slice(qt * TT, (qt + 1) * TT)
        tasks = []
        for oc in range(DC):
            def t_wo(oc=oc):
                b = misc_bank()
                mm(cx, cx.ps[b][:, :], wo_s[hs][:, oc * 128:(oc + 1) * 128], on[osl], True, True,
                   reads=[("wo_s", hs), ("on", osl)], writes=[("ps", b)])
                rs = cx.res[:, oc, ts]
                p.add("vector", lambda e: e.tensor_tensor(out=rs, in0=rs, in1=cx.ps[b][:, :], op=ALU.add),
                      reads=[("ps", b), ("res", oc, qt)], writes=[("res", oc, qt)])
            tasks.append(t_wo)
        return tasks

    def flush(lst):
        while lst:
            lst.pop(0)()

    nq = 0
    for h in range(NH):
        hs = h % 2
        p.add("sync", lambda e, hs=hs, h=h: e.dma_start(out=wuq_s[hs].rearrange("p c n -> p (c n)"), in_=cx.wbf["wuq"][j][h]),
              reads=[("wuq_d", j)], writes=[("wuq_s", hs)], dsem=("wuq_s", hs))
        p.add("sync", lambda e, hs=hs, h=h: e.dma_start(out=wo_s[hs], in_=cx.wbf["wo"][j][:, h * D:(h + 1) * D]),
              reads=[("wo_d", j)], writes=[("wo_s", hs)], dsem=("wo_s", hs))
        q_loads(h, 0, nq % 2)
        kvb = [0, 1, 6, 7]
        for t in range(NT):
            ts = slice(t * TT, (t + 1) * TT)
            b = kvb[(2 * t) % 4]
            mm(cx, cx.ps[b][:, :], wukv[:, h, 0:128], ckvn[:, ts], True, True,
               reads=["wukv_s", ("ckvn", t)], writes=[("ps", b)])
            p.add("scalar", lambda e, ts=ts, b=b: e.activation(out=kn[:, ts], in_=cx.ps[b][:, :], func=AF.Copy),
                  reads=[("ps", b)], writes=[("kn", t)])
            b = kvb[(2 * t + 1) % 4]
            for i in range(4):
                kc = t * 4 + i
                mm(cx, cx.ps[b][:, i * 128:(i + 1) * 128], ckvn[:, kc * 128:(kc + 1) * 128], wukv[:, h, 128:256], True, True,
                   reads=["wukv_s", ("ckvn", t)], writes=[("ps", b)])
            p.add("vector", lambda e, t=t, b=b: e.tensor_copy(out=V[:, t * 4:(t + 1) * 4, :], in_=cx.ps[b][:, :].rearrange("p (c n) -> p c n", c=4)),
                  reads=[("ps", b)], writes=[("V", t)])
        flush(q_tasks(h, nq % 2))
        for qt in range(NT):
            qs = nq % 2
            osl = nq % 2
            nq += 1
            if qt + 1 < NT:
                q_loads(h, qt + 1, nq % 2)
                pq.extend(q_tasks(h, nq % 2))
            nk = 4 * (qt + 1)
            ob = 2 + osl
            db = 4 + osl

            def s_mm(kc):
                jd = kc - 4 * qt
                c0 = 128 * jd if jd > 0 else 0
                sb = kc % 2
                mm(cx, cx.ps[sb][:, c0:TT], kn[:, kc * 128:(kc + 1) * 128], qn[qs][:, c0:TT], True, False,
                   reads=[("kn", kc // 4), ("qn", qs)], writes=[("ps", sb)])
                mm(cx, cx.ps[sb][:, c0:TT], krope[:, kc * 128:(kc + 1) * 128], qr[qs][:, c0:TT], False, jd < 0,
                   reads=[("krope", kc // 4), ("qr", qs)], writes=[("ps", sb)])
                if jd >= 0:
                    mm(cx, cx.ps[sb][:, c0:c0 + 128], cx.ident[:, :], cx.tri[:, :], False, True,
                       reads=["ident", "tri"], writes=[("ps", sb)])

            for kc in range(min(2, nk)):
                s_mm(kc)
            for kc in range(nk):
                jd = kc - 4 * qt
                c0 = 128 * jd if jd > 0 else 0
                sb = kc % 2
                pt = PT[st["npt"] % NP]
                ptk = ("PT", st["npt"] % NP)
                st["npt"] += 1
                p.add("scalar", lambda e, pt=pt, sb=sb, c0=c0: e.activation(out=pt[:, c0:TT], in_=cx.ps[sb][:, c0:TT], func=AF.Exp, scale=sm_scale),
                      reads=[("ps", sb)], writes=[ptk])
                mm(cx, cx.ps[ob][:, c0:TT], V[:, kc, :], pt[:, c0:TT], kc == 0, kc == nk - 1,
                   reads=[("V", kc // 4), ptk], writes=[("ps", ob)])
                mm(cx, cx.ps[db][:, c0:TT], cx.ones[:, :], pt[:, c0:TT], kc == 0, kc == nk - 1,
                   reads=["ones", ptk], writes=[("ps", db)])
                if kc + 2 < nk:
                    s_mm(kc + 2)
                if kc == 1 and st.get("tail") is not None:
                    tl = st["tail"]
                    st["tail"] = None
                    tl()
                if pq:
                    pq.pop(0)()
                elif pw:
                    pw.pop(0)()
            flush(pq)
            while len(pw) > DC:
                pw.pop(0)()
            def tail(h=h, qt=qt, osl=osl, ob=ob, db=db):
                rd = rden[osl]
                p.add("scalar", lambda e: e.activation(out=rd, in_=cx.ps[db][:, :], func=AF.Ln),
                      reads=[("ps", db)], writes=[("rden", osl)])
                p.add("scalar", lambda e: e.activation(out=rd, in_=rd, func=AF.Exp, scale=-1.0),
                      reads=[("rden", osl)], writes=[("rden", osl)])
                p.add("vector", lambda e: e.tensor_tensor(out=on[osl], in0=cx.ps[ob][:, :], in1=rd, op=ALU.mult),
                      reads=[("ps", ob), ("rden", osl)], writes=[("on", osl)])
                pw.extend(wo_tasks(h, qt, osl))

            st["tail"] = tail
    if st.get("tail") is not None:
        st["tail"]()
        st["tail"] = None
    flush(pw)
    ar.reset(m0)


def final_phase(cx, do_norm):
    p = cx.p
    p.barrier()
    ar = cx.arena
    m0 = ar.mark()
    sq = ar.bf16(DC * TT).rearrange("p (c n) -> p c n", c=DC)
    rstd = ar.f32(TT)
    ob = [ar.f32(TT) for _ in range(4)]
    gcol = cx.vcol["norm_final"]
    n = 0
    outk = []
    for t in range(cx.NT):
        ts = slice(t * TT, (t + 1) * TT)
        if do_norm:
            norm_squares(cx, lambda c, ts=ts: cx.res[:, c, ts], [("res", c, t) for c in range(DC)], DC, sq, "sq")
            norm_rstd(cx, DC, sq, "sq", 6, rstd, "rstd", float(D))
        for c in range(DC):
            if do_norm:
                o = ob[n % 4]
                ok = ("ob", n % 4)
                n += 1
                p.add("vector", lambda e, c=c, o=o, ts=ts: e.scalar_tensor_tensor(out=o, in0=cx.res[:, c, ts], scalar=cx.vecs[:, gcol + c:gcol + c + 1], in1=rstd,
                                                                               op0=ALU.mult, op1=ALU.mult),
                      reads=[("res", c, t), "rstd", "vecs"], writes=[ok])
                p.add("sync", lambda e, c=c, o=o, ts=ts: e.dma_start(out=cx.outT[c * 128:(c + 1) * 128, ts], in_=o),
                      reads=[ok], writes=[("out", c, t)], dsem=("outw", n % 4))
            else:
                p.add("sync", lambda e, c=c, ts=ts: e.dma_start(out=cx.outT[c * 128:(c + 1) * 128, ts], in_=cx.res[:, c, ts]),
                      reads=[("res", c, t)], writes=[("out", c, t)], dsem=("outw", c % 4))
            outk.append(("out", c, t))
    p.add("sync", None, reads=outk)
    ar.reset(m0)


def vec_layout():
    col = 0
    vcol = {"norm_mix": [], "norm_ffn": [], "pool_scale": [], "q_norm": [], "kv_norm": []}
    for l in range(DEPTH):
        vcol["norm_mix"].append(col); col += DC
        vcol["norm_ffn"].append(col); col += DC
    for j in range(2):
        vcol["pool_scale"].append(col); col += DC
    for j in range(2):
        vcol["q_norm"].append(col); col += 3
        vcol["kv_norm"].append(col); col += 1
    vcol["norm_final"] = col; col += DC
    return vcol, col


def build_program(S, layers, do_final_norm):
    nc = bass.Bass("TRN2", target_bir_lowering=False)
    NT = S // TT
    vcol, NV = vec_layout()
    cx = Cx()
    cx.S = S
    cx.NT = NT
    cx.vcol = vcol
    xT = nc.dram_tensor("xT", [D, S], F32, kind="ExternalInput").ap()
    cx.pos = nc.dram_tensor("pos", [1, S], I32, kind="ExternalInput").ap()
    vecs_d = nc.dram_tensor("vecs", [128, NV], F32, kind="ExternalInput").ap()
    cst_d = nc.dram_tensor("cst", [128, 288], F32, kind="ExternalInput").ap()
    cx.outT = nc.dram_tensor("outT", [D, S], F32, kind="ExternalOutput").ap()
    pool_layers = [l for l in layers if l % 2 == 0]
    mla_layers = [l for l in layers if l % 2 == 1]
    w32 = {}
    cx.wbf = {"wgu": {}, "wd": {}, "wp": {}, "wdn": {}, "wuq": {}, "wukv": {}, "wo": {}}

    def wpair(name, shape):
        a = nc.dram_tensor(name, shape, F32, kind="ExternalInput").ap()
        b = nc.dram_tensor(name + "_bf", shape, BF16, kind="Internal").ap()
        return a, b

    for l in layers:
        w32[("wgu", l)], cx.wbf["wgu"][l] = wpair("wgu%d" % l, [FC, 128, DC * 2 * 128])
        w32[("wd", l)], cx.wbf["wd"][l] = wpair("wd%d" % l, [DC, 128, FC * 128])
    for l in pool_layers:
        j = l // 2
        w32[("wp", j)], cx.wbf["wp"][j] = wpair("wp%d" % j, [128, 2048])
    for l in mla_layers:
        j = l // 2
        w32[("wdn", j)], cx.wbf["wdn"][j] = wpair("wdn%d" % j, [128, DC * 640])
        w32[("wuq", j)], cx.wbf["wuq"][j] = wpair("wuq%d" % j, [NH, 128, 768])
        w32[("wukv", j)], cx.wbf["wukv"][j] = wpair("wukv%d" % j, [128, 2048])
        w32[("wo", j)], cx.wbf["wo"][j] = wpair("wo%d" % j, [128, NH * D])
    if mla_layers:
        cx.tab = [nc.dram_tensor("tab%d" % i, [64, S], F32, kind="Internal").ap() for i in range(2)]
        cx.cqn_d = nc.dram_tensor("cqn_d", [NT, 128, 3 * TT], BF16, kind="Internal").ap()

    with contextlib.ExitStack() as st:
        p = Prog(nc, st)
        cx.p = p
        res_t = st.enter_context(nc.sbuf_tensor("res", [128, DC * S], F32))
        cx.res = res_t[:, :].rearrange("p (c n) -> p c n", c=DC)
        cx.vecs = st.enter_context(nc.sbuf_tensor("vecs_sb", [128, NV], F32))[:, :]
        cst = st.enter_context(nc.sbuf_tensor("cst_sb", [128, 288], F32))[:, :]
        cx.ones = st.enter_context(nc.sbuf_tensor("ones_bf", [128, 128], BF16))[:, :]
        cx.tri = st.enter_context(nc.sbuf_tensor("tri_bf", [128, 128], BF16))[:, :]
        cx.ident = st.enter_context(nc.sbuf_tensor("ident_bf", [128, 128], BF16))[:, :]
        cx.epsc = st.enter_context(nc.sbuf_tensor("epsc", [128, 1], F32))[:, :]
        cx.invn = cst[:, 0:16]
        cx.cst64 = cst[0:64, 16:18]
        tri32 = cst[:, 32:160]
        remaining = nc.sbuf_bytes_remaining
        A = (remaining - 64) // 4
        cx.arena = Arena(st.enter_context(nc.sbuf_tensor("arena", [128, A], F32))[:, :], A)
        cx.ps = [st.enter_context(nc.psum_tensor("ps%d" % i, [128, TT], F32)) for i in range(8)]

        p.add("sync", lambda e: e.dma_start(out=cx.vecs, in_=vecs_d), writes=["vecs"], dsem="vecs")
        p.add("sync", lambda e: e.dma_start(out=cst, in_=cst_d), writes=["invn", "cst64", "tri32"], dsem="cst")
        p.add("vector", lambda e: e.memset(cx.ones, 1.0), writes=["ones"])
        p.add("vector", lambda e: e.memset(cx.epsc, EPS), writes=["epsc"])
        p.add("vector", lambda e: e.tensor_scalar(out=cx.tri, in0=tri32, scalar1=-1.0, scalar2=30000.0, op0=ALU.add, op1=ALU.mult),
              reads=["tri32"], writes=["tri"])
        p.add("vector", lambda e: e.tensor_copy(out=cx.ident, in_=cst[:, 160:288]), reads=["tri32"], writes=["ident"])
        cx.arena_full = A
        if mla_layers:
            R = S + 5 * TT
            top = Arena(cx.arena.ap[:, A - R:A], R)
            cx.arena.n = A - R
            posall = top.i32(S, 64)
            p.add("sync", lambda e: e.dma_start(out=posall, in_=cx.pos.partition_broadcast(64)), writes=["posall"], dsem="posall")
        res_loads = []
        for c in range(DC):
            res_loads.append(p.add("sync", lambda e, c=c: e.dma_start(out=cx.res[:, c, :], in_=xT[c * 128:(c + 1) * 128, :]),
                                   writes=[("res", c, t) for t in range(NT)], dsem=("resld", c)))
        def cvt_now(dst, src, key, after):
            p.add("gpsimd", lambda e: e.dma_start(out=dst, in_=src), writes=[key], dsem=("cv", key), nowaw=True, after=after)

        def layer_casts(l):
            j = l // 2
            out = []
            if l % 2 == 0:
                out.append((cx.wbf["wp"][j], w32[("wp", j)], ("wp_d", j)))
            else:
                out.append((cx.wbf["wdn"][j], w32[("wdn", j)], ("wdn_d", j)))
                out.append((cx.wbf["wukv"][j], w32[("wukv", j)], ("wukv_d", j)))
                for h in range(NH):
                    out.append((cx.wbf["wuq"][j][h], w32[("wuq", j)][h], ("wuq_d", j)))
                for h in range(NH):
                    out.append((cx.wbf["wo"][j][:, h * D:(h + 1) * D], w32[("wo", j)][:, h * D:(h + 1) * D], ("wo_d", j)))
            nmix = len(out)
            for fc in range(FC):
                out.append((cx.wbf["wgu"][l][fc], w32[("wgu", l)][fc], ("wgu_d", l, fc // 11)))
            for jj in range(DC):
                out.append((cx.wbf["wd"][l][jj], w32[("wd", l)][jj], ("wd_d", l)))
            return out, nmix

        tab_ops = []
        if mla_layers:
            tab_ops = rope_tables(cx, top, posall)
            if layers[0] % 2 == 1:
                p.barrier()
                cx.arena.n = A
        c0, nmix = layer_casts(layers[0])
        for i, (dst, src, key) in enumerate(c0):
            cvt_now(dst, src, key, list(res_loads) if i < nmix else list(res_loads) + tab_ops)
        cx.pending_casts = []
        cast_plan = {}
        for li in range(1, len(layers)):
            cl, _ = layer_casts(layers[li])
            cast_plan[layers[li - 1]] = [(lambda after, d=d, s_=s_, k=k: cvt_now(d, s_, k, after)) for (d, s_, k) in cl]
        for l in layers:
            if l % 2 == 0:
                pool_phase(cx, l, l // 2)
            else:
                mla_phase(cx, l, l // 2)
            cx.pending_casts = cast_plan.get(l, [])
            ffn_phase(cx, l)
            while cx.pending_casts:
                cx.pending_casts.pop(0)([])
        final_phase(cx, do_final_norm)
        p.emit()
    return nc


def host_consts():
    cst = np.zeros((128, 288), np.float32)
    cst[:, 0:16] = (1.0 / np.arange(1, 17, dtype=np.float32))[None, :]
    inv = (1.0 / (10000.0 ** (np.arange(0, ROPE, 2, dtype=np.float32) / ROPE))).astype(np.float32)
    cst[0:64, 16] = np.concatenate([inv, inv])
    cst[0:32, 17] = -1.0
    cst[32:64, 17] = 1.0
    k = np.arange(128)[:, None]
    q = np.arange(128)[None, :]
    cst[:, 32:160] = (k <= q).astype(np.float32)
    cst[:, 160:288] = np.eye(128, dtype=np.float32)
    return cst


def col_layout(v):
    n = v.shape[0] // 128
    return np.ascontiguousarray(v.reshape(n, 128).T)


def host_weights(inp, layers):
    vcol, NV = vec_layout()
    vecs = np.zeros((128, NV), np.float32)
    for l in range(DEPTH):
        vecs[:, vcol["norm_mix"][l]:vcol["norm_mix"][l] + DC] = col_layout(inp["norm_mix"][l])
        vecs[:, vcol["norm_ffn"][l]:vcol["norm_ffn"][l] + DC] = col_layout(inp["norm_ffn"][l])
    for j in range(2):
        vecs[:, vcol["pool_scale"][j]:vcol["pool_scale"][j] + DC] = col_layout(inp["pool_scale"][j])
        vecs[:, vcol["q_norm"][j]:vcol["q_norm"][j] + 3] = col_layout(inp["mla_q_norm"][j])
        vecs[:, vcol["kv_norm"][j]:vcol["kv_norm"][j] + 1] = col_layout(inp["mla_kv_norm"][j])
    vecs[:, vcol["norm_final"]:vcol["norm_final"] + DC] = col_layout(inp["norm_final"])
    w = {"vecs": vecs, "cst": host_consts()}
    for l in layers:
        wg = inp["ffn_w_gate"][l].reshape(DC, 128, FC, 128)
        wu = inp["ffn_w_up"][l].reshape(DC, 128, FC, 128)
        wgu = np.stack([wg, wu], axis=3)
        w["wgu%d" % l] = np.ascontiguousarray(wgu.transpose(2, 1, 0, 3, 4)).reshape(FC, 128, DC * 2 * 128)
        wd = inp["ffn_w_down"][l].reshape(FC, 128, DC, 128)
        w["wd%d" % l] = np.ascontiguousarray(wd.transpose(2, 1, 0, 3)).reshape(DC, 128, FC * 128)
        j = l // 2
        if l % 2 == 0:
            wp = inp["pool_w"][j].reshape(4, 2, 128, 256)
            w["wp%d" % j] = np.ascontiguousarray(wp.transpose(2, 0, 1, 3)).reshape(128, 2048)
        else:
            wdn = inp["mla_w_down"][j]
            kr = wdn[:, QL + KVL:]
            krs = np.concatenate([kr[:, 32:], kr[:, :32]], axis=1)
            wdn_aug = np.concatenate([wdn, krs], axis=1)
            w["wdn%d" % j] = np.ascontiguousarray(wdn_aug.reshape(DC, 128, 640).transpose(1, 0, 2)).reshape(128, DC * 640)
            wuq = inp["mla_w_uq"][j].reshape(3, 128, NH, 192)
            qr = wuq[..., 128:]
            qrs = np.concatenate([qr[..., 32:], qr[..., :32]], axis=-1)
            wuq_aug = np.concatenate([wuq, qrs], axis=-1)
            w["wuq%d" % j] = np.ascontiguousarray(wuq_aug.transpose(2, 1, 0, 3)).reshape(NH, 128, 768)
            w["wukv%d" % j] = np.ascontiguousarray(inp["mla_w_ukv"][j])
            wo = inp["mla_w_o"][j].reshape(NH, 128, D)
            w["wo%d" % j] = np.ascontiguousarray(wo.transpose(1, 0, 2)).reshape(128, NH * D)
    return w


_PROG_CACHE = {}


def run_layers(x_fm, positions, inp, layers, do_final_norm, n_cores):
    S = x_fm.shape[2]
    key = (S, tuple(layers), do_final_norm)
    if key not in _PROG_CACHE:
        _PROG_CACHE[key] = build_program(S, list(layers), do_final_norm)
    nc = _PROG_CACHE[key]
    w = host_weights(inp, layers)
    in_maps = []
    for b in range(n_cores):
        m = dict(w)
        m["xT"] = np.ascontiguousarray(x_fm[b])
        m["pos"] = np.ascontiguousarray(positions[b:b + 1].astype(np.int32))
        in_maps.append(m)
    res = run_bass_kernel_spmd(nc, in_maps, core_ids=list(range(n_cores)))
    return np.stack([r["outT"] for r in res.results], axis=0)


def kernel(**inputs):
    inp = {k: np.asarray(v) for k, v in inputs.items()}
    x = inp["x"].astype(np.float32, copy=False)
    B = x.shape[0]
    x_fm = np.ascontiguousarray(x.transpose(0, 2, 1))
    out_fm = run_layers(x_fm, inp["positions"], inp, [0, 1, 2, 3], True, B)
    return np.ascontiguousarray(out_fm.transpose(0, 2, 1))
```

```python
import contextlib
import math
import numpy as np
import concourse.bass as bass
import concourse.mybir as mybir
from concourse.bass_utils import run_bass_kernel_spmd

F32 = mybir.dt.float32
BF16 = mybir.dt.bfloat16
I32 = mybir.dt.int32
ALU = mybir.AluOpType
AF = mybir.ActivationFunctionType

D = 1024
DC = 8
DFF = 2816
FC = 22
NH = 8
QL = 384
KVL = 128
ROPE = 64
TT = 512
EPS = 1e-6
DEPTH = 4
EPOCH = 20000
N_CORES = 8


class Op:
    __slots__ = ("eng", "fn", "deps", "needs_inc", "sig", "is_dma", "dsem")


class Prog:
    ENGS = ["sync", "tensor", "vector", "scalar", "gpsimd"]

    def __init__(self, nc, stack):
        self.nc = nc
        self.stack = stack
        self.ops = {e: [] for e in self.ENGS}
        self.lastw = {}
        self.readers = {}
        self.dma_sems = {}
        self.eng_sems = {}
        self.pending = {}

    def barrier(self):
        deps = []
        for e in self.ENGS:
            for op in reversed(self.ops[e]):
                if not op.is_dma and op.fn is not None:
                    deps.append(op)
                    break
        seen = set()
        for e in self.ENGS:
            for op in reversed(self.ops[e]):
                if op.is_dma and op.dsem not in seen:
                    seen.add(op.dsem)
                    if not (isinstance(op.dsem, tuple) and op.dsem[0] == "cv"):
                        deps.append(op)
        self.pending = {e: list(deps) for e in self.ENGS}

    def _new_sem(self, name):
        return self.stack.enter_context(self.nc.semaphore(name))

    def add(self, eng, fn, reads=(), writes=(), dsem=None, nowaw=False, after=()):
        op = Op()
        op.eng = eng
        op.fn = fn
        op.deps = set()
        op.needs_inc = False
        op.sig = None
        op.is_dma = dsem is not None
        op.dsem = dsem
        for k in reads:
            w = self.lastw.get(k)
            if w is not None:
                op.deps.add(w)
        for k in writes:
            w = self.lastw.get(k)
            if w is not None and (w.is_dma or w.eng != eng) and not nowaw:
                op.deps.add(w)
            for r in self.readers.get(k, ()):
                if r is not op and (r.is_dma or r.eng != eng):
                    op.deps.add(r)
        if eng == "tensor":
            op.deps = {d for d in op.deps if d.is_dma or d.eng != "tensor"}
        op.deps.update(after)
        pend = self.pending.pop(eng, None)
        if pend:
            op.deps.update(d for d in pend if d.is_dma or d.eng != eng or eng != "tensor")
        for d in op.deps:
            d.needs_inc = True
        for k in reads:
            self.readers.setdefault(k, []).append(op)
        for k in writes:
            self.lastw[k] = op
            self.readers[k] = []
        if op.is_dma:
            ent = self.dma_sems.get(dsem)
            if ent is None:
                ent = [self._new_sem("d%d_%s" % (len(self.dma_sems), "".join(ch for ch in str(dsem) if ch.isalnum()))), 0]
                self.dma_sems[dsem] = ent
            ent[1] += 16
            op.sig = (ent[0], ent[1])
        self.ops[eng].append(op)
        return op

    def _assign(self):
        for e in self.ENGS:
            cnt = 0
            for op in self.ops[e]:
                if op.is_dma or not op.needs_inc:
                    continue
                ep = cnt // EPOCH
                key = (e, ep)
                if key not in self.eng_sems:
                    self.eng_sems[key] = self._new_sem("c_%s_%d" % (e, ep))
                cnt += 1
                op.sig = (self.eng_sems[key], cnt - ep * EPOCH)

    def emit(self):
        self._assign()
        nc = self.nc
        with nc.Block() as block:
            for e in self.ENGS:
                ops = self.ops[e]

                def body(eng, ops=ops):
                    waited = {}
                    for op in ops:
                        need = {}
                        for d in op.deps:
                            sem, val = d.sig
                            if need.get(id(sem), (None, 0))[1] < val:
                                need[id(sem)] = (sem, val)
                        for sid, (sem, val) in need.items():
                            if waited.get(sid, 0) < val:
                                eng.wait_ge(sem, val)
                                waited[sid] = val
                        if op.fn is None:
                            continue
                        ins = op.fn(eng)
                        if op.is_dma:
                            ins.then_inc(op.sig[0], 16)
                        elif op.needs_inc:
                            ins.then_inc(op.sig[0], 1)

                getattr(block, e)(body)


class Arena:
    def __init__(self, ap, nwords):
        self.ap = ap
        self.n = nwords
        self.off = 0

    def mark(self):
        return self.off

    def reset(self, m):
        self.off = m

    def f32(self, ncols, parts=128):
        a = self.ap[0:parts, self.off:self.off + ncols]
        self.off += ncols
        assert self.off <= self.n, ("arena overflow", self.off, self.n)
        return a

    def bf16(self, ncols, parts=128):
        w = (ncols + 1) // 2
        a = self.ap[0:parts, self.off:self.off + w].bitcast(BF16)
        self.off += w
        assert self.off <= self.n, ("arena overflow", self.off, self.n)
        return a

    def i32(self, ncols, parts=128):
        a = self.ap[0:parts, self.off:self.off + ncols].bitcast(I32)
        self.off += ncols
        assert self.off <= self.n
        return a


class Cx:
    pass


def mm(cx, out, lhsT, rhs, start, stop, reads, writes):
    return cx.p.add("tensor", lambda e: e.matmul(out, lhsT, rhs, start=start, stop=stop), reads=reads, writes=writes)


def norm_squares(cx, src_fn, src_keys, nch, sq, sqkey):
    p = cx.p
    for c in range(nch):
        p.add("scalar", lambda e, c=c: e.activation(out=sq[:, c, :], in_=src_fn(c), func=AF.Square),
              reads=[src_keys[c]], writes=[(sqkey, c)])


def norm_rstd(cx, nch, sq, sqkey, bank, rstd, rstdkey, dim):
    p = cx.p
    for c in range(nch):
        mm(cx, cx.ps[bank][:, :], cx.ones[:, :], sq[:, c, :], c == 0, c == nch - 1,
           reads=[(sqkey, c), "ones"], writes=[("ps", bank)])
    p.add("scalar", lambda e: e.activation(out=rstd, in_=cx.ps[bank][:, :], func=AF.Ln, bias=cx.epsc, scale=1.0 / dim),
          reads=[("ps", bank), "epsc"], writes=[rstdkey])
    p.add("scalar", lambda e: e.activation(out=rstd, in_=rstd, func=AF.Exp, scale=-0.5), reads=[rstdkey], writes=[rstdkey])


def norm_apply_res(cx, t, gcol0, rstd, rstdkey, hT, hkey):
    p = cx.p
    for c in range(DC):
        p.add("vector",
              lambda e, c=c: e.scalar_tensor_tensor(out=hT[:, c, :], in0=cx.res[:, c, t * TT:(t + 1) * TT],
                                                    scalar=cx.vecs[:, gcol0 + c:gcol0 + c + 1], in1=rstd,
                                                    op0=ALU.mult, op1=ALU.mult),
              reads=[("res", c, t), rstdkey, "vecs"], writes=[(hkey, c)])


def ffn_phase(cx, l):
    p = cx.p
    p.barrier()
    cx.arena.n = cx.arena_full
    ar = cx.arena
    m0 = ar.mark()
    NT = cx.NT
    sq = ar.bf16(DC * TT).rearrange("p (c n) -> p c n", c=DC)
    rstd = ar.f32(TT)
    hT = ar.bf16(DC * TT).rearrange("p (c n) -> p c n", c=DC)
    gT = ar.bf16(FC * TT).rearrange("p (c n) -> p c n", c=FC)
    stmp = [ar.f32(TT) for _ in range(2)]
    NW1, NW2 = 4, 3
    wgu_s = [ar.bf16(DC * 2 * 128).rearrange("p (c g n) -> p c g n", c=DC, g=2) for _ in range(NW1)]
    wd_s = [ar.bf16(FC * 128).rearrange("p (c n) -> p c n", c=FC) for _ in range(NW2)]
    gcol = cx.vcol["norm_ffn"][l]
    wgu_d = cx.wbf["wgu"][l]
    wd_d = cx.wbf["wd"][l]
    uid = ("ffn", l)

    def squares(t):
        norm_squares(cx, lambda c, t=t: cx.res[:, c, t * TT:(t + 1) * TT], [("res", c, t) for c in range(DC)], DC, sq, "sq")

    def rest(t):
        norm_rstd(cx, DC, sq, "sq", 6, rstd, "rstd", float(D))
        norm_apply_res(cx, t, gcol, rstd, "rstd", hT, "hT")

    squares(0)
    rest(0)
    n1 = 0
    n2 = 0
    for t in range(NT):
        for fc in range(FC):
            s = n1 % NW1
            n1 += 1
            p.add("sync", lambda e, s=s, fc=fc: e.dma_start(out=wgu_s[s].rearrange("p c g n -> p (c g n)"), in_=wgu_d[fc]),
                  reads=[("wgu_d", l, fc // 11)], writes=[("wgu_s", s)], dsem=("wgu_s", s))
            bg = fc % 2
            bu = 2 + fc % 2
            for dc in range(DC):
                mop = mm(cx, cx.ps[bg][:, :], wgu_s[s][:, dc, 0, :], hT[:, dc, :], dc == 0, dc == DC - 1,
                         reads=[("wgu_s", s), ("hT", dc)], writes=[("ps", bg)])
                if dc == 0 and cx.pending_casts and (t * FC + fc) % 3 == 0:
                    cx.pending_casts.pop(0)([mop])
            for dc in range(DC):
                mm(cx, cx.ps[bu][:, :], wgu_s[s][:, dc, 1, :], hT[:, dc, :], dc == 0, dc == DC - 1,
                   reads=[("wgu_s", s), ("hT", dc)], writes=[("ps", bu)])
            st = stmp[fc % 2]
            p.add("scalar", lambda e, st=st, bg=bg: e.activation(out=st, in_=cx.ps[bg][:, :], func=AF.Silu),
                  reads=[("ps", bg)], writes=[("stmp", fc % 2)])
            p.add("vector", lambda e, st=st, bu=bu, fc=fc: e.tensor_tensor(out=gT[:, fc, :], in0=st, in1=cx.ps[bu][:, :], op=ALU.mult),
                  reads=[("stmp", fc % 2), ("ps", bu)], writes=[("gT", fc)])
            if fc == 11 and t + 1 < NT:
                squares(t + 1)
        if t + 1 < NT:
            rest(t + 1)
        for j in range(DC):
            s = n2 % NW2
            n2 += 1
            p.add("sync", lambda e, s=s, j=j: e.dma_start(out=wd_s[s].rearrange("p c n -> p (c n)"), in_=wd_d[j]),
                  reads=[("wd_d", l)], writes=[("wd_s", s)], dsem=("wd_s", s))
            b = 4 + j % 2
            for fc in range(FC):
                mm(cx, cx.ps[b][:, :], wd_s[s][:, fc, :], gT[:, fc, :], fc == 0, fc == FC - 1,
                   reads=[("wd_s", s), ("gT", fc)], writes=[("ps", b)])
            rs = cx.res[:, j, t * TT:(t + 1) * TT]
            p.add("vector", lambda e, rs=rs, b=b: e.tensor_tensor(out=rs, in0=rs, in1=cx.ps[b][:, :], op=ALU.add),
                  reads=[("ps", b), ("res", j, t)], writes=[("res", j, t)])
    ar.reset(m0)


POOL_WINDOWS = (2, 4, 8, 16)
HALO = 16


def pool_phase(cx, l, j):
    p = cx.p
    p.barrier()
    ar = cx.arena
    m0 = ar.mark()
    NT = cx.NT
    W = HALO + TT
    sq = ar.bf16(DC * TT).rearrange("p (c n) -> p c n", c=DC)
    rstd = ar.f32(TT)
    hb = ar.f32(DC * W).rearrange("p (c n) -> p c n", c=DC)
    NA = 4
    ab = [ar.f32(W) for _ in range(NA)]
    pT = ar.bf16(DC * TT).rearrange("p (c n) -> p c n", c=DC)
    wp = ar.bf16(4 * 2 * 256).rearrange("p (g c n) -> p g c n", g=4, c=2)
    ftmp = ar.f32(HALO)
    gcol = cx.vcol["norm_mix"][l]
    scol = cx.vcol["pool_scale"][j]
    p.add("sync", lambda e: e.dma_start(out=wp.rearrange("p g c n -> p (g c n)"), in_=cx.wbf["wp"][j]),
          reads=[("wp_d", j)], writes=["wp_s"], dsem="wp_s")
    for c in range(DC):
        p.add("vector", lambda e, c=c: e.memset(hb[:, c, 0:HALO], 0.0), writes=[("hb", c)])
    na = 0
    for t in range(NT):
        norm_squares(cx, lambda c, t=t: cx.res[:, c, t * TT:(t + 1) * TT], [("res", c, t) for c in range(DC)], DC, sq, "sq")
        norm_rstd(cx, DC, sq, "sq", 6, rstd, "rstd", float(D))
        for c in range(DC):
            p.add("vector",
                  lambda e, c=c, t=t: e.scalar_tensor_tensor(out=hb[:, c, HALO:W], in0=cx.res[:, c, t * TT:(t + 1) * TT],
                                                        scalar=cx.vecs[:, gcol + c:gcol + c + 1], in1=rstd,
                                                        op0=ALU.mult, op1=ALU.mult),
                  reads=[("res", c, t), "rstd", "vecs"], writes=[("hb", c)])
        for c in range(DC):
            g = c // 2
            win = POOL_WINDOWS[g]
            nlev = g + 1
            src = hb[:, c, :]
            srck = ("hb", c)
            lo = 0
            for lev in range(nlev):
                sh = 1 << lev
                lo = lo + sh
                a = ab[na % NA]
                ak = ("ab", na % NA)
                na += 1
                p.add("vector", lambda e, a=a, src=src, lo=lo, sh=sh: e.tensor_tensor(out=a[:, lo:W], in0=src[:, lo:W], in1=src[:, lo - sh:W - sh], op=ALU.add),
                      reads=[srck], writes=[ak])
                src = a
                srck = ak
            p.add("vector", lambda e, src=src, c=c, win=win: e.scalar_tensor_tensor(out=pT[:, c, :], in0=src[:, HALO:W], scalar=1.0 / win, in1=hb[:, c, HALO:W],
                                                                                op0=ALU.mult, op1=ALU.subtract),
                  reads=[srck, ("hb", c)], writes=[("pT", c)])
            if t == 0:
                n = win - 1
                p.add("vector", lambda e, src=src, n=n: e.tensor_tensor(out=ftmp[:, 0:n], in0=src[:, HALO:HALO + n], in1=cx.invn[:, 0:n], op=ALU.mult),
                      reads=[srck, "invn"], writes=["ftmp"])
                p.add("vector", lambda e, c=c, n=n: e.tensor_tensor(out=pT[:, c, 0:n], in0=ftmp[:, 0:n], in1=hb[:, c, HALO:HALO + n], op=ALU.subtract),
                      reads=["ftmp", ("hb", c)], writes=[("pT", c)])
        if t + 1 < NT:
            for c in range(DC):
                p.add("vector", lambda e, c=c: e.tensor_copy(out=hb[:, c, 0:HALO], in_=hb[:, c, TT:W]),
                      reads=[("hb", c)], writes=[("hb", c)])
        for oc in range(DC):
            g = oc // 2
            o2 = oc % 2
            b = oc % 4
            for cc in range(2):
                mm(cx, cx.ps[b][:, :], wp[:, g, cc, o2 * 128:(o2 + 1) * 128], pT[:, 2 * g + cc, :], cc == 0, cc == 1,
                   reads=["wp_s", ("pT", 2 * g + cc)], writes=[("ps", b)])
            rs = cx.res[:, oc, t * TT:(t + 1) * TT]
            p.add("vector", lambda e, rs=rs, b=b, oc=oc: e.scalar_tensor_tensor(out=rs, in0=cx.ps[b][:, :], scalar=cx.vecs[:, scol + oc:scol + oc + 1], in1=rs,
                                                                                op0=ALU.mult, op1=ALU.add),
                  reads=[("ps", b), ("res", oc, t), "vecs"], writes=[("res", oc, t)])
    ar.reset(m0)


def rope_tables(cx, top, posall):
    p = cx.p
    ar = top
    S = cx.S
    C1 = 6.28125
    C2 = 2 * math.pi - 6.28125
    ang, kf, sn, cs = ar.f32(TT, 64), ar.f32(TT, 64), ar.f32(TT, 64), ar.f32(TT, 64)
    ki = ar.i32(TT, 64)
    tab_ops = []
    for t in range(cx.NT):
        posi = posall[:, t * TT:(t + 1) * TT]
        u = ("rt", 0)
        p.add("vector", lambda e, ang=ang, posi=posi: e.tensor_copy(out=ang, in_=posi), reads=["posall"], writes=[(u, "ang")])
        p.add("vector", lambda e, ang=ang: e.tensor_scalar(out=ang, in0=ang, scalar1=cx.cst64[:, 0:1], scalar2=None, op0=ALU.mult),
              reads=[(u, "ang"), "cst64"], writes=[(u, "ang")])
        p.add("vector", lambda e, ang=ang, kf=kf: e.tensor_single_scalar(out=kf, in_=ang, scalar=1.0 / (2 * math.pi), op=ALU.mult),
              reads=[(u, "ang")], writes=[(u, "kf")])
        p.add("vector", lambda e, ki=ki, kf=kf: e.tensor_copy(out=ki, in_=kf), reads=[(u, "kf")], writes=[(u, "ki")])
        p.add("vector", lambda e, ki=ki, kf=kf: e.tensor_copy(out=kf, in_=ki), reads=[(u, "ki")], writes=[(u, "kf")])
        p.add("vector", lambda e, ang=ang, kf=kf: e.scalar_tensor_tensor(out=ang, in0=kf, scalar=-C1, in1=ang, op0=ALU.mult, op1=ALU.add),
              reads=[(u, "kf"), (u, "ang")], writes=[(u, "ang")])
        p.add("vector", lambda e, ang=ang, kf=kf: e.scalar_tensor_tensor(out=ang, in0=kf, scalar=-C2, in1=ang, op0=ALU.mult, op1=ALU.add),
              reads=[(u, "kf"), (u, "ang")], writes=[(u, "ang")])
        p.add("vector", lambda e, ang=ang, kf=kf: e.tensor_scalar(out=kf, in0=ang, scalar1=math.pi, scalar2=-2 * math.pi, op0=ALU.is_gt, op1=ALU.mult),
              reads=[(u, "ang")], writes=[(u, "kf")])
        p.add("vector", lambda e, ang=ang, kf=kf: e.tensor_tensor(out=ang, in0=ang, in1=kf, op=ALU.add),
              reads=[(u, "ang"), (u, "kf")], writes=[(u, "ang")])
        p.add("scalar", lambda e, ang=ang, sn=sn: e.activation(out=sn, in_=ang, func=AF.Sin), reads=[(u, "ang")], writes=[(u, "sn")])
        p.add("vector", lambda e, sn=sn: e.tensor_scalar(out=sn, in0=sn, scalar1=cx.cst64[:, 1:2], scalar2=None, op0=ALU.mult),
              reads=[(u, "sn"), "cst64"], writes=[(u, "sn")])
        p.add("vector", lambda e, ang=ang: e.tensor_single_scalar(out=ang, in_=ang, scalar=math.pi / 2, op=ALU.add),
              reads=[(u, "ang"), (u, "sn")], writes=[(u, "ang")])
        p.add("vector", lambda e, ang=ang, kf=kf: e.tensor_scalar(out=kf, in0=ang, scalar1=math.pi, scalar2=-2 * math.pi, op0=ALU.is_gt, op1=ALU.mult),
              reads=[(u, "ang")], writes=[(u, "kf")])
        p.add("vector", lambda e, ang=ang, kf=kf: e.tensor_tensor(out=ang, in0=ang, in1=kf, op=ALU.add),
              reads=[(u, "ang"), (u, "kf")], writes=[(u, "ang")])
        p.add("scalar", lambda e, ang=ang, cs=cs: e.activation(out=cs, in_=ang, func=AF.Sin), reads=[(u, "ang")], writes=[(u, "cs")])
        tab_ops.append(p.add("sync", lambda e, cs=cs, t=t: e.dma_start(out=cx.tab[0][:, t * TT:(t + 1) * TT], in_=cs),
                             reads=[(u, "cs")], writes=[("tab", t, 0)], dsem=("tabw", t, 0)))
        tab_ops.append(p.add("sync", lambda e, sn=sn, t=t: e.dma_start(out=cx.tab[1][:, t * TT:(t + 1) * TT], in_=sn),
                             reads=[(u, "sn")], writes=[("tab", t, 1)], dsem=("tabw", t, 1)))
    return tab_ops


def mla_phase(cx, l, j):
    p = cx.p
    p.barrier()
    cx.arena.n = cx.arena_full
    ar = cx.arena
    m0 = ar.mark()
    NT = cx.NT
    S = cx.S
    NKC = S // 128
    sm_scale = float((128 + ROPE) ** -0.5)
    ckvn = ar.bf16(S)
    krope = ar.bf16(S)
    m1 = ar.mark()
    sq = ar.bf16(DC * TT).rearrange("p (c n) -> p c n", c=DC)
    rstd = ar.f32(TT)
    rstd2 = ar.f32(TT)
    hT = ar.bf16(DC * TT).rearrange("p (c n) -> p c n", c=DC)
    cq_sb = ar.f32(3 * TT).rearrange("p (c n) -> p c n", c=3)
    ckv_sb = ar.f32(TT)
    cqn_t = [ar.bf16(3 * TT).rearrange("p (c n) -> p c n", c=3) for _ in range(2)]
    tabA = [ar.f32(2 * TT, 64).rearrange("p (c n) -> p c n", c=2) for _ in range(2)]
    rt1 = ar.f32(TT, 64)
    rt2 = ar.f32(TT, 64)
    wdn = ar.bf16(DC * 640).rearrange("p (c n) -> p c n", c=DC)
    gcol = cx.vcol["norm_mix"][l]
    qcol = cx.vcol["q_norm"][j]
    kvcol = cx.vcol["kv_norm"][j]
    p.add("vector", lambda e: e.memset(krope[64:128, :], 0.0), writes=[("krope", t) for t in range(NT)])
    p.add("sync", lambda e: e.dma_start(out=wdn.rearrange("p c n -> p (c n)"), in_=cx.wbf["wdn"][j]),
          reads=[("wdn_d", j)], writes=["wdn_s"], dsem="wdn_s")
    sq2 = ar.bf16(4 * TT).rearrange("p (c n) -> p c n", c=4)
    rstd3 = rstd2

    def stage1(t):
        ts = slice(t * TT, (t + 1) * TT)
        norm_squares(cx, lambda c, ts=ts: cx.res[:, c, ts], [("res", c, t) for c in range(DC)], DC, sq, "sq")
        norm_rstd(cx, DC, sq, "sq", 6, rstd, "rstd", float(D))
        norm_apply_res(cx, t, gcol, rstd, "rstd", hT, "hT")

    stage1(0)
    for t in range(NT):
        ts = slice(t * TT, (t + 1) * TT)
        tb = tabA[t % 2]
        p.add("sync", lambda e, tb=tb, ts=ts: e.dma_start(out=tb[:, 0, :], in_=cx.tab[0][:, ts]),
              reads=[("tab", t, 0), ("tab", t, 1)], writes=[("tabA", t % 2)], dsem=("tabA", t % 2))
        p.add("sync", lambda e, tb=tb, ts=ts: e.dma_start(out=tb[:, 1, :], in_=cx.tab[1][:, ts]),
              reads=[("tab", t, 0), ("tab", t, 1)], writes=[("tabA", t % 2)], dsem=("tabA", t % 2))
        outs = [(0, 0, 128), (1, 128, 128), (2, 256, 128), (3, 384, 128), (4, 512, 64), (5, 576, 64)]
        for (b, c0, m) in outs:
            for dc in range(DC):
                mm(cx, cx.ps[b][0:m, :], wdn[:, dc, c0:c0 + m], hT[:, dc, :], dc == 0, dc == DC - 1,
                   reads=["wdn_s", ("hT", dc)], writes=[("ps", b)])
        if t + 1 < NT:
            stage1(t + 1)
        for c in range(3):
            p.add("scalar", lambda e, c=c: e.activation(out=cq_sb[:, c, :], in_=cx.ps[c][:, :], func=AF.Copy),
                  reads=[("ps", c)], writes=[("cq_sb", c)])
        p.add("scalar", lambda e: e.activation(out=ckv_sb, in_=cx.ps[3][:, :], func=AF.Copy),
              reads=[("ps", 3)], writes=["ckv_sb"])
        p.add("vector", lambda e, tb=tb: e.tensor_tensor(out=rt1, in0=cx.ps[4][0:64, :], in1=tb[:, 0, :], op=ALU.mult),
              reads=[("ps", 4), ("tabA", t % 2)], writes=["rt1"])
        p.add("vector", lambda e, tb=tb: e.tensor_tensor(out=rt2, in0=cx.ps[5][0:64, :], in1=tb[:, 1, :], op=ALU.mult),
              reads=[("ps", 5), ("tabA", t % 2)], writes=["rt2"])
        p.add("vector", lambda e, ts=ts: e.tensor_tensor(out=krope[0:64, ts], in0=rt1, in1=rt2, op=ALU.add),
              reads=["rt1", "rt2"], writes=[("krope", t)])
        norm_squares(cx, lambda c: cq_sb[:, c, :], [("cq_sb", c) for c in range(3)], 3, sq2, "sq2")
        norm_rstd(cx, 3, sq2, "sq2", 7, rstd2, "rstd2", float(QL))
        cqt = cqn_t[t % 2]
        for c in range(3):
            p.add("vector", lambda e, c=c, cqt=cqt: e.scalar_tensor_tensor(out=cqt[:, c, :], in0=cq_sb[:, c, :], scalar=cx.vecs[:, qcol + c:qcol + c + 1], in1=rstd2,
                                                                          op0=ALU.mult, op1=ALU.mult),
                  reads=[("cq_sb", c), "rstd2", "vecs"], writes=[("cqn_t", t % 2)])
        p.add("sync", lambda e, cqt=cqt, t=t: e.dma_start(out=cx.cqn_d[t], in_=cqt.rearrange("p c n -> p (c n)")),
              reads=[("cqn_t", t % 2)], writes=[("cqn_d", t)], dsem=("cqn_w", t % 2))
        p.add("scalar", lambda e: e.activation(out=sq2[:, 3, :], in_=ckv_sb, func=AF.Square), reads=["ckv_sb"], writes=[("sq2", 3)])
        mm(cx, cx.ps[7][:, :], cx.ones[:, :], sq2[:, 3, :], True, True, reads=[("sq2", 3), "ones"], writes=[("ps", 7)])
        p.add("scalar", lambda e: e.activation(out=rstd3, in_=cx.ps[7][:, :], func=AF.Ln, bias=cx.epsc, scale=1.0 / KVL),
              reads=[("ps", 7), "epsc"], writes=["rstd2"])
        p.add("scalar", lambda e: e.activation(out=rstd3, in_=rstd3, func=AF.Exp, scale=-0.5), reads=["rstd2"], writes=["rstd2"])
        p.add("vector", lambda e, ts=ts: e.scalar_tensor_tensor(out=ckvn[:, ts], in0=ckv_sb, scalar=cx.vecs[:, kvcol:kvcol + 1], in1=rstd3,
                                                                op0=ALU.mult, op1=ALU.mult),
              reads=["ckv_sb", "rstd2", "vecs"], writes=[("ckvn", t)])
    ar.reset(m1)
    p.barrier()
    kn = ar.bf16(S)
    V = ar.bf16(NKC * 128).rearrange("p (c n) -> p c n", c=NKC)
    qn = [ar.bf16(TT) for _ in range(2)]
    qr = [ar.bf16(TT) for _ in range(2)]
    tabB = [ar.f32(2 * TT, 64).rearrange("p (c n) -> p c n", c=2) for _ in range(2)]
    NP = 4
    PT = [ar.bf16(TT) for _ in range(NP)]
    cqs = [ar.bf16(3 * TT).rearrange("p (c n) -> p c n", c=3) for _ in range(2)]
    wuq_s = [ar.bf16(3 * 256).rearrange("p (c n) -> p c n", c=3) for _ in range(2)]
    wukv = ar.bf16(NH * 256).rearrange("p (h n) -> p h n", h=NH)
    wo_s = [ar.bf16(D) for _ in range(2)]
    on = [ar.bf16(TT) for _ in range(2)]
    rden = [ar.f32(TT) for _ in range(2)]
    qt1 = ar.f32(TT, 64)
    qt2 = ar.f32(TT, 64)
    p.add("sync", lambda e: e.dma_start(out=wukv.rearrange("p h n -> p (h n)"), in_=cx.wbf["wukv"][j]),
          reads=[("wukv_d", j)], writes=["wukv_s"], dsem="wukv_s")
    for i in range(2):
        p.add("vector", lambda e, i=i: e.memset(qr[i][64:128, :], 0.0), writes=[("qr", i)])
    st = {"npt": 0, "nmisc": 0}
    pq = []
    pw = []

    def misc_bank():
        b = 6 + st["nmisc"] % 2
        st["nmisc"] += 1
        return b

    def q_loads(h, qt, qs):
        ts = slice(qt * TT, (qt + 1) * TT)
        p.add("sync", lambda e: e.dma_start(out=cqs[qs].rearrange("p c n -> p (c n)"), in_=cx.cqn_d[qt]),
              reads=[("cqn_d", qt)], writes=[("cqs", qs)], dsem=("cqs", qs))
        tb = tabB[qs]
        p.add("sync", lambda e: e.dma_start(out=tb[:, 0, :], in_=cx.tab[0][:, ts]),
              reads=[("tab", qt, 0)], writes=[("tabB", qs, 0)], dsem=("tabB", qs, 0))
        p.add("sync", lambda e: e.dma_start(out=tb[:, 1, :], in_=cx.tab[1][:, ts]),
              reads=[("tab", qt, 1)], writes=[("tabB", qs, 1)], dsem=("tabB", qs, 1))

    def q_tasks(h, qs):
        hs = h % 2
        tb = tabB[qs]

        def t_qn():
            b = misc_bank()
            for c in range(3):
                mm(cx, cx.ps[b][:, :], wuq_s[hs][:, c, 0:128], cqs[qs][:, c, :], c == 0, c == 2,
                   reads=[("wuq_s", hs), ("cqs", qs)], writes=[("ps", b)])
            p.add("vector", lambda e: e.tensor_copy(out=qn[qs], in_=cx.ps[b][:, :]),
                  reads=[("ps", b)], writes=[("qn", qs)])

        def t_qa():
            b = misc_bank()
            for c in range(3):
                mm(cx, cx.ps[b][0:64, :], wuq_s[hs][:, c, 128:192], cqs[qs][:, c, :], c == 0, c == 2,
                   reads=[("wuq_s", hs), ("cqs", qs)], writes=[("ps", b)])
            p.add("vector", lambda e: e.tensor_tensor(out=qt1, in0=cx.ps[b][0:64, :], in1=tb[:, 0, :], op=ALU.mult),
                  reads=[("ps", b), ("tabB", qs, 0)], writes=["qt1"])

        def t_qb():
            b = misc_bank()
            for c in range(3):
                mm(cx, cx.ps[b][0:64, :], wuq_s[hs][:, c, 192:256], cqs[qs][:, c, :], c == 0, c == 2,
                   reads=[("wuq_s", hs), ("cqs", qs)], writes=[("ps", b)])
            p.add("vector", lambda e: e.tensor_tensor(out=qt2, in0=cx.ps[b][0:64, :], in1=tb[:, 1, :], op=ALU.mult),
                  reads=[("ps", b), ("tabB", qs, 1)], writes=["qt2"])
            p.add("vector", lambda e: e.tensor_tensor(out=qr[qs][0:64, :], in0=qt1, in1=qt2, op=ALU.add),
                  reads=["qt1", "qt2"], writes=[("qr", qs)])

        return [t_qn, t_qa, t_qb]

    def wo_tasks(h, qt, osl):
        hs = h % 2
        ts = slice(qt * TT, (qt + 1) * TT)
        tasks = []
        for oc in range(DC):
            def t_wo(oc=oc):
                b = misc_bank()
                mm(cx, cx.ps[b][:, :], wo_s[hs][:, oc * 128:(oc + 1) * 128], on[osl], True, True,
                   reads=[("wo_s", hs), ("on", osl)], writes=[("ps", b)])
                rs = cx.res[:, oc, ts]
                p.add("vector", lambda e: e.tensor_tensor(out=rs, in0=rs, in1=cx.ps[b][:, :], op=ALU.add),
                      reads=[("ps", b), ("res", oc, qt)], writes=[("res", oc, qt)])
            tasks.append(t_wo)
        return tasks

    def flush(lst):
        while lst:
            lst.pop(0)()

    def head_loads(h):
        hs = h % 2
        p.add("sync", lambda e: e.dma_start(out=wuq_s[hs].rearrange("p c n -> p (c n)"), in_=cx.wbf["wuq"][j][h]),
              reads=[("wuq_d", j)], writes=[("wuq_s", hs)], dsem=("wuq_s", hs))
        p.add("sync", lambda e: e.dma_start(out=wo_s[hs], in_=cx.wbf["wo"][j][:, h * D:(h + 1) * D]),
              reads=[("wo_d", j)], writes=[("wo_s", hs)], dsem=("wo_s", hs))

    nq = 0
    head_loads(0)
    q_loads(0, 0, 0)
    for h in range(NH):
        hs = h % 2
        kvb = [0, 1, 6, 7]
        for t in range(NT):
            ts = slice(t * TT, (t + 1) * TT)
            b = kvb[(2 * t) % 4]
            mm(cx, cx.ps[b][:, :], wukv[:, h, 0:128], ckvn[:, ts], True, True,
               reads=["wukv_s", ("ckvn", t)], writes=[("ps", b)])
            p.add("scalar", lambda e, ts=ts, b=b: e.activation(out=kn[:, ts], in_=cx.ps[b][:, :], func=AF.Copy),
                  reads=[("ps", b)], writes=[("kn", t)])
            b = kvb[(2 * t + 1) % 4]
            for i in range(4):
                kc = t * 4 + i
                mm(cx, cx.ps[b][:, i * 128:(i + 1) * 128], ckvn[:, kc * 128:(kc + 1) * 128], wukv[:, h, 128:256], True, True,
                   reads=["wukv_s", ("ckvn", t)], writes=[("ps", b)])
            p.add("vector", lambda e, t=t, b=b: e.tensor_copy(out=V[:, t * 4:(t + 1) * 4, :], in_=cx.ps[b][:, :].rearrange("p (c n) -> p c n", c=4)),
                  reads=[("ps", b)], writes=[("V", t)])
        if h == 0:
            flush(q_tasks(0, nq % 2))
        for qt in range(NT):
            qs = nq % 2
            osl = nq % 2
            nq += 1
            if qt + 1 < NT:
                q_loads(h, qt + 1, nq % 2)
                pq.extend(q_tasks(h, nq % 2))
            elif h + 1 < NH:
                flush(pw)
                head_loads(h + 1)
                q_loads(h + 1, 0, nq % 2)
                pq.extend(q_tasks(h + 1, nq % 2))
            nk = 4 * (qt + 1)
            ob = 2 + osl
            db = 4 + osl

            def s_mm(kc):
                jd = kc - 4 * qt
                c0 = 128 * jd if jd > 0 else 0
                sb = kc % 2
                mm(cx, cx.ps[sb][:, c0:TT], kn[:, kc * 128:(kc + 1) * 128], qn[qs][:, c0:TT], True, False,
                   reads=[("kn", kc // 4), ("qn", qs)], writes=[("ps", sb)])
                mm(cx, cx.ps[sb][:, c0:TT], krope[:, kc * 128:(kc + 1) * 128], qr[qs][:, c0:TT], False, jd < 0,
                   reads=[("krope", kc // 4), ("qr", qs)], writes=[("ps", sb)])
                if jd >= 0:
                    mm(cx, cx.ps[sb][:, c0:c0 + 128], cx.ident[:, :], cx.tri[:, :], False, True,
                       reads=["ident", "tri"], writes=[("ps", sb)])

            for kc in range(min(2, nk)):
                s_mm(kc)
            for kc in range(nk):
                jd = kc - 4 * qt
                c0 = 128 * jd if jd > 0 else 0
                sb = kc % 2
                pt = PT[st["npt"] % NP]
                ptk = ("PT", st["npt"] % NP)
                st["npt"] += 1
                p.add("scalar", lambda e, pt=pt, sb=sb, c0=c0: e.activation(out=pt[:, c0:TT], in_=cx.ps[sb][:, c0:TT], func=AF.Exp, scale=sm_scale),
                      reads=[("ps", sb)], writes=[ptk])
                mm(cx, cx.ps[ob][:, c0:TT], V[:, kc, :], pt[:, c0:TT], kc == 0, kc == nk - 1,
                   reads=[("V", kc // 4), ptk], writes=[("ps", ob)])
                mm(cx, cx.ps[db][:, c0:TT], cx.ones[:, :], pt[:, c0:TT], kc == 0, kc == nk - 1,
                   reads=["ones", ptk], writes=[("ps", db)])
                if kc + 2 < nk:
                    s_mm(kc + 2)
                if kc == 1 and st.get("tail") is not None:
                    tl = st["tail"]
                    st["tail"] = None
                    tl()
                if pq:
                    pq.pop(0)()
                elif pw:
                    pw.pop(0)()
            flush(pq)
            while len(pw) > DC:
                pw.pop(0)()
            def tail(h=h, qt=qt, osl=osl, ob=ob, db=db):
                rd = rden[osl]
                p.add("scalar", lambda e: e.activation(out=rd, in_=cx.ps[db][:, :], func=AF.Ln),
                      reads=[("ps", db)], writes=[("rden", osl)])
                p.add("scalar", lambda e: e.activation(out=rd, in_=rd, func=AF.Exp, scale=-1.0),
                      reads=[("rden", osl)], writes=[("rden", osl)])
                p.add("vector", lambda e: e.tensor_tensor(out=on[osl], in0=cx.ps[ob][:, :], in1=rd, op=ALU.mult),
                      reads=[("ps", ob), ("rden", osl)], writes=[("on", osl)])
                pw.extend(wo_tasks(h, qt, osl))

            st["tail"] = tail
    if st.get("tail") is not None:
        st["tail"]()
        st["tail"] = None
    flush(pw)
    ar.reset(m0)


def final_phase(cx, do_norm):
    p = cx.p
    p.barrier()
    ar = cx.arena
    m0 = ar.mark()
    sq = ar.bf16(DC * TT).rearrange("p (c n) -> p c n", c=DC)
    rstd = ar.f32(TT)
    ob = [ar.f32(TT) for _ in range(4)]
    gcol = cx.vcol["norm_final"]
    n = 0
    outk = []
    for t in range(cx.NT):
        ts = slice(t * TT, (t + 1) * TT)
        if do_norm:
            norm_squares(cx, lambda c, ts=ts: cx.res[:, c, ts], [("res", c, t) for c in range(DC)], DC, sq, "sq")
            norm_rstd(cx, DC, sq, "sq", 6, rstd, "rstd", float(D))
        for c in range(DC):
            if do_norm:
                o = ob[n % 4]
                ok = ("ob", n % 4)
                n += 1
                p.add("vector", lambda e, c=c, o=o, ts=ts: e.scalar_tensor_tensor(out=o, in0=cx.res[:, c, ts], scalar=cx.vecs[:, gcol + c:gcol + c + 1], in1=rstd,
                                                                               op0=ALU.mult, op1=ALU.mult),
                      reads=[("res", c, t), "rstd", "vecs"], writes=[ok])
                p.add("sync", lambda e, c=c, o=o, ts=ts: e.dma_start(out=cx.outT[c * 128:(c + 1) * 128, ts], in_=o),
                      reads=[ok], writes=[("out", c, t)], dsem=("outw", n % 4))
            else:
                p.add("sync", lambda e, c=c, ts=ts: e.dma_start(out=cx.outT[c * 128:(c + 1) * 128, ts], in_=cx.res[:, c, ts]),
                      reads=[("res", c, t)], writes=[("out", c, t)], dsem=("outw", c % 4))
            outk.append(("out", c, t))
    p.add("sync", None, reads=outk)
    ar.reset(m0)


def vec_layout():
    col = 0
    vcol = {"norm_mix": [], "norm_ffn": [], "pool_scale": [], "q_norm": [], "kv_norm": []}
    for l in range(DEPTH):
        vcol["norm_mix"].append(col); col += DC
        vcol["norm_ffn"].append(col); col += DC
    for j in range(2):
        vcol["pool_scale"].append(col); col += DC
    for j in range(2):
        vcol["q_norm"].append(col); col += 3
        vcol["kv_norm"].append(col); col += 1
    vcol["norm_final"] = col; col += DC
    return vcol, col


def build_program(S, layers, do_final_norm):
    nc = bass.Bass("TRN2", target_bir_lowering=False)
    NT = S // TT
    vcol, NV = vec_layout()
    cx = Cx()
    cx.S = S
    cx.NT = NT
    cx.vcol = vcol
    xT = nc.dram_tensor("xT", [D, S], F32, kind="ExternalInput").ap()
    cx.pos = nc.dram_tensor("pos", [1, S], I32, kind="ExternalInput").ap()
    vecs_d = nc.dram_tensor("vecs", [128, NV], F32, kind="ExternalInput").ap()
    cst_d = nc.dram_tensor("cst", [128, 288], F32, kind="ExternalInput").ap()
    cx.outT = nc.dram_tensor("outT", [D, S], F32, kind="ExternalOutput").ap()
    pool_layers = [l for l in layers if l % 2 == 0]
    mla_layers = [l for l in layers if l % 2 == 1]
    w32 = {}
    cx.wbf = {"wgu": {}, "wd": {}, "wp": {}, "wdn": {}, "wuq": {}, "wukv": {}, "wo": {}}

    def wpair(name, shape):
        a = nc.dram_tensor(name, shape, F32, kind="ExternalInput").ap()
        b = nc.dram_tensor(name + "_bf", shape, BF16, kind="Internal").ap()
        return a, b

    for l in layers:
        w32[("wgu", l)], cx.wbf["wgu"][l] = wpair("wgu%d" % l, [FC, 128, DC * 2 * 128])
        w32[("wd", l)], cx.wbf["wd"][l] = wpair("wd%d" % l, [DC, 128, FC * 128])
    for l in pool_layers:
        j = l // 2
        w32[("wp", j)], cx.wbf["wp"][j] = wpair("wp%d" % j, [128, 2048])
    for l in mla_layers:
        j = l // 2
        w32[("wdn", j)], cx.wbf["wdn"][j] = wpair("wdn%d" % j, [128, DC * 640])
        w32[("wuq", j)], cx.wbf["wuq"][j] = wpair("wuq%d" % j, [NH, 128, 768])
        w32[("wukv", j)], cx.wbf["wukv"][j] = wpair("wukv%d" % j, [128, 2048])
        w32[("wo", j)], cx.wbf["wo"][j] = wpair("wo%d" % j, [128, NH * D])
    if mla_layers:
        cx.tab = [nc.dram_tensor("tab%d" % i, [64, S], F32, kind="Internal").ap() for i in range(2)]
        cx.cqn_d = nc.dram_tensor("cqn_d", [NT, 128, 3 * TT], BF16, kind="Internal").ap()

    with contextlib.ExitStack() as st:
        p = Prog(nc, st)
        cx.p = p
        res_t = st.enter_context(nc.sbuf_tensor("res", [128, DC * S], F32))
        cx.res = res_t[:, :].rearrange("p (c n) -> p c n", c=DC)
        cx.vecs = st.enter_context(nc.sbuf_tensor("vecs_sb", [128, NV], F32))[:, :]
        cst = st.enter_context(nc.sbuf_tensor("cst_sb", [128, 288], F32))[:, :]
        cx.ones = st.enter_context(nc.sbuf_tensor("ones_bf", [128, 128], BF16))[:, :]
        cx.tri = st.enter_context(nc.sbuf_tensor("tri_bf", [128, 128], BF16))[:, :]
        cx.ident = st.enter_context(nc.sbuf_tensor("ident_bf", [128, 128], BF16))[:, :]
        cx.epsc = st.enter_context(nc.sbuf_tensor("epsc", [128, 1], F32))[:, :]
        cx.invn = cst[:, 0:16]
        cx.cst64 = cst[0:64, 16:18]
        tri32 = cst[:, 32:160]
        remaining = nc.sbuf_bytes_remaining
        A = (remaining - 64) // 4
        cx.arena = Arena(st.enter_context(nc.sbuf_tensor("arena", [128, A], F32))[:, :], A)
        cx.ps = [st.enter_context(nc.psum_tensor("ps%d" % i, [128, TT], F32)) for i in range(8)]

        p.add("sync", lambda e: e.dma_start(out=cx.vecs, in_=vecs_d), writes=["vecs"], dsem="vecs")
        p.add("sync", lambda e: e.dma_start(out=cst, in_=cst_d), writes=["invn", "cst64", "tri32"], dsem="cst")
        p.add("vector", lambda e: e.memset(cx.ones, 1.0), writes=["ones"])
        p.add("vector", lambda e: e.memset(cx.epsc, EPS), writes=["epsc"])
        p.add("vector", lambda e: e.tensor_scalar(out=cx.tri, in0=tri32, scalar1=-1.0, scalar2=30000.0, op0=ALU.add, op1=ALU.mult),
              reads=["tri32"], writes=["tri"])
        p.add("vector", lambda e: e.tensor_copy(out=cx.ident, in_=cst[:, 160:288]), reads=["tri32"], writes=["ident"])
        cx.arena_full = A
        if mla_layers:
            R = S + 5 * TT
            top = Arena(cx.arena.ap[:, A - R:A], R)
            cx.arena.n = A - R
            posall = top.i32(S, 64)
            p.add("sync", lambda e: e.dma_start(out=posall, in_=cx.pos.partition_broadcast(64)), writes=["posall"], dsem="posall")
        res_loads = []
        for c in range(DC):
            res_loads.append(p.add("sync", lambda e, c=c: e.dma_start(out=cx.res[:, c, :], in_=xT[c * 128:(c + 1) * 128, :]),
                                   writes=[("res", c, t) for t in range(NT)], dsem=("resld", c)))
        def cvt_now(dst, src, key, after):
            p.add("gpsimd", lambda e: e.dma_start(out=dst, in_=src), writes=[key], dsem=("cv", key), nowaw=True, after=after)

        def layer_casts(l):
            j = l // 2
            out = []
            if l % 2 == 0:
                out.append((cx.wbf["wp"][j], w32[("wp", j)], ("wp_d", j)))
            else:
                out.append((cx.wbf["wdn"][j], w32[("wdn", j)], ("wdn_d", j)))
                out.append((cx.wbf["wukv"][j], w32[("wukv", j)], ("wukv_d", j)))
                for h in range(NH):
                    out.append((cx.wbf["wuq"][j][h], w32[("wuq", j)][h], ("wuq_d", j)))
                for h in range(NH):
                    out.append((cx.wbf["wo"][j][:, h * D:(h + 1) * D], w32[("wo", j)][:, h * D:(h + 1) * D], ("wo_d", j)))
            nmix = len(out)
            for fc in range(FC):
                out.append((cx.wbf["wgu"][l][fc], w32[("wgu", l)][fc], ("wgu_d", l, fc // 11)))
            for jj in range(DC):
                out.append((cx.wbf["wd"][l][jj], w32[("wd", l)][jj], ("wd_d", l)))
            return out, nmix

        tab_ops = []
        if mla_layers:
            tab_ops = rope_tables(cx, top, posall)
            if layers[0] % 2 == 1:
                p.barrier()
                cx.arena.n = A
        c0, nmix = layer_casts(layers[0])
        for i, (dst, src, key) in enumerate(c0):
            cvt_now(dst, src, key, list(res_loads) if i < nmix else list(res_loads) + tab_ops)
        cx.pending_casts = []
        cast_plan = {}
        for li in range(1, len(layers)):
            cl, _ = layer_casts(layers[li])
            cast_plan[layers[li - 1]] = [(lambda after, d=d, s_=s_, k=k: cvt_now(d, s_, k, after)) for (d, s_, k) in cl]
        for l in layers:
            if l % 2 == 0:
                pool_phase(cx, l, l // 2)
            else:
                mla_phase(cx, l, l // 2)
            cx.pending_casts = cast_plan.get(l, [])
            ffn_phase(cx, l)
            while cx.pending_casts:
                cx.pending_casts.pop(0)([])
        final_phase(cx, do_final_norm)
        p.emit()
    return nc


def host_consts():
    cst = np.zeros((128, 288), np.float32)
    cst[:, 0:16] = (1.0 / np.arange(1, 17, dtype=np.float32))[None, :]
    inv = (1.0 / (10000.0 ** (np.arange(0, ROPE, 2, dtype=np.float32) / ROPE))).astype(np.float32)
    cst[0:64, 16] = np.concatenate([inv, inv])
    cst[0:32, 17] = -1.0
    cst[32:64, 17] = 1.0
    k = np.arange(128)[:, None]
    q = np.arange(128)[None, :]
    cst[:, 32:160] = (k <= q).astype(np.float32)
    cst[:, 160:288] = np.eye(128, dtype=np.float32)
    return cst


def col_layout(v):
    n = v.shape[0] // 128
    return np.ascontiguousarray(v.reshape(n, 128).T)


def host_weights(inp, layers):
    vcol, NV = vec_layout()
    vecs = np.zeros((128, NV), np.float32)
    for l in range(DEPTH):
        vecs[:, vcol["norm_mix"][l]:vcol["norm_mix"][l] + DC] = col_layout(inp["norm_mix"][l])
        vecs[:, vcol["norm_ffn"][l]:vcol["norm_ffn"][l] + DC] = col_layout(inp["norm_ffn"][l])
    for j in range(2):
        vecs[:, vcol["pool_scale"][j]:vcol["pool_scale"][j] + DC] = col_layout(inp["pool_scale"][j])
        vecs[:, vcol["q_norm"][j]:vcol["q_norm"][j] + 3] = col_layout(inp["mla_q_norm"][j])
        vecs[:, vcol["kv_norm"][j]:vcol["kv_norm"][j] + 1] = col_layout(inp["mla_kv_norm"][j])
    vecs[:, vcol["norm_final"]:vcol["norm_final"] + DC] = col_layout(inp["norm_final"])
    w = {"vecs": vecs, "cst": host_consts()}
    for l in layers:
        wg = inp["ffn_w_gate"][l].reshape(DC, 128, FC, 128)
        wu = inp["ffn_w_up"][l].reshape(DC, 128, FC, 128)
        wgu = np.stack([wg, wu], axis=3)
        w["wgu%d" % l] = np.ascontiguousarray(wgu.transpose(2, 1, 0, 3, 4)).reshape(FC, 128, DC * 2 * 128)
        wd = inp["ffn_w_down"][l].reshape(FC, 128, DC, 128)
        w["wd%d" % l] = np.ascontiguousarray(wd.transpose(2, 1, 0, 3)).reshape(DC, 128, FC * 128)
        j = l // 2
        if l % 2 == 0:
            wp = inp["pool_w"][j].reshape(4, 2, 128, 256)
            w["wp%d" % j] = np.ascontiguousarray(wp.transpose(2, 0, 1, 3)).reshape(128, 2048)
        else:
            wdn = inp["mla_w_down"][j]
            kr = wdn[:, QL + KVL:]
            krs = np.concatenate([kr[:, 32:], kr[:, :32]], axis=1)
            wdn_aug = np.concatenate([wdn, krs], axis=1)
            w["wdn%d" % j] = np.ascontiguousarray(wdn_aug.reshape(DC, 128, 640).transpose(1, 0, 2)).reshape(128, DC * 640)
            wuq = inp["mla_w_uq"][j].reshape(3, 128, NH, 192)
            qr = wuq[..., 128:]
            qrs = np.concatenate([qr[..., 32:], qr[..., :32]], axis=-1)
            wuq_aug = np.concatenate([wuq, qrs], axis=-1)
            w["wuq%d" % j] = np.ascontiguousarray(wuq_aug.transpose(2, 1, 0, 3)).reshape(NH, 128, 768)
            w["wukv%d" % j] = np.ascontiguousarray(inp["mla_w_ukv"][j])
            wo = inp["mla_w_o"][j].reshape(NH, 128, D)
            w["wo%d" % j] = np.ascontiguousarray(wo.transpose(1, 0, 2)).reshape(128, NH * D)
    return w


_PROG_CACHE = {}


def run_layers(x_fm, positions, inp, layers, do_final_norm, n_cores):
    S = x_fm.shape[2]
    key = (S, tuple(layers), do_final_norm)
    if key not in _PROG_CACHE:
        _PROG_CACHE[key] = build_program(S, list(layers), do_final_norm)
    nc = _PROG_CACHE[key]
    w = host_weights(inp, layers)
    in_maps = []
    for b in range(n_cores):
        m = dict(w)
        m["xT"] = np.ascontiguousarray(x_fm[b])
        m["pos"] = np.ascontiguousarray(positions[b:b + 1].astype(np.int32))
        in_maps.append(m)
    res = run_bass_kernel_spmd(nc, in_maps, core_ids=list(range(n_cores)))
    return np.stack([r["outT"] for r in res.results], axis=0)


def kernel(**inputs):
    inp = {k: np.asarray(v) for k, v in inputs.items()}
    x = inp["x"].astype(np.float32, copy=False)
    B = x.shape[0]
    x_fm = np.ascontiguousarray(x.transpose(0, 2, 1))
    out_fm = run_layers(x_fm, inp["positions"], inp, [0, 1, 2, 3], True, B)
    return np.ascontiguousarray(out_fm.transpose(0, 2, 1))
```
